# Optimizing a Trainium2 kernel written in Bass

```python
import math
import jax, jax.numpy as jnp
from jax import lax
import numpy as np

D_MODEL = 1024
BATCH = 8
SEQ = 8192
DEPTH = 2

HEAD_DIM = 64
N_SLOTS = D_MODEL // HEAD_DIM
SB_HEADS = N_SLOTS // 2
SB_WIDTH = SB_HEADS * HEAD_DIM
DIFF_HEADS = N_SLOTS // 4
DIFF_VDIM = 2 * HEAD_DIM
DIFF_WIDTH = DIFF_HEADS * DIFF_VDIM
DIFF_QK_WIDTH = DIFF_HEADS * 2 * HEAD_DIM
EVEN_IN = 3 * SB_WIDTH + 2 * DIFF_QK_WIDTH + DIFF_WIDTH
DIL_HEADS = N_SLOTS
ODD_IN = 3 * DIL_HEADS * HEAD_DIM
DIL_BRANCHES = ((128, 1), (512, 4), (2048, 16))
W_MAX = 2048
Q_BLOCK = 128
NUM_BUCKETS = 32
MAX_DISTANCE = 128
D_FF = 4 * D_MODEL
PLE_DIM = 256
N_EVEN = (DEPTH + 1) // 2
N_ODD = DEPTH // 2
NORM_EPS = 1e-6
SUBLN_EPS = 1e-5
NEG = -1e30

kernel_name = "hybrid_stickbreak_diff_dilated_trunk"


def rmsnorm(x, g, eps=NORM_EPS):
    x32 = x.astype(jnp.float32)
    y = x32 * lax.rsqrt(jnp.mean(x32 * x32, axis=-1, keepdims=True) + eps) * g.astype(jnp.float32)
    return y.astype(x.dtype)


def t5_bucket(dist):
    n = jnp.maximum(dist, 0)
    max_exact = NUM_BUCKETS // 2
    nf = jnp.maximum(n, 1).astype(jnp.float32)
    large = max_exact + (jnp.log(nf / max_exact) / math.log(MAX_DISTANCE / max_exact)
                         * (NUM_BUCKETS - max_exact)).astype(jnp.int32)
    large = jnp.minimum(large, NUM_BUCKETS - 1)
    return jnp.where(n < max_exact, n, large)


def even_mixer(h, w_in, w_out, lq1, lk1, lq2, lk2, subln_g, t5_table, layer_idx):
    B, S, _ = h.shape
    f32 = jnp.float32
    proj = h @ w_in
    cuts = np.cumsum([SB_WIDTH, SB_WIDTH, SB_WIDTH, DIFF_QK_WIDTH, DIFF_QK_WIDTH]).tolist()
    qa, ka, va, qb, kb, vb = jnp.split(proj, cuts, axis=-1)
    qa = qa.reshape(B, S, SB_HEADS, HEAD_DIM).astype(f32)
    ka = ka.reshape(B, S, SB_HEADS, HEAD_DIM).astype(f32)
    va = va.reshape(B, S, SB_HEADS, HEAD_DIM).astype(f32)
    qb = qb.reshape(B, S, DIFF_HEADS, 2, HEAD_DIM).astype(f32)
    kb = kb.reshape(B, S, DIFF_HEADS, 2, HEAD_DIM).astype(f32)
    vb = vb.reshape(B, S, DIFF_HEADS, DIFF_VDIM).astype(f32)
    scale = HEAD_DIM ** -0.5
    lambda_init = 0.8 - 0.6 * math.exp(-0.3 * layer_idx)
    lam = (jnp.exp(jnp.sum(lq1.astype(f32) * lk1.astype(f32)))
           - jnp.exp(jnp.sum(lq2.astype(f32) * lk2.astype(f32))) + lambda_init)
    table = t5_table.astype(f32)
    kpos = jnp.arange(S)

    def block(i):
        t0 = i * Q_BLOCK
        qpos = t0 + jnp.arange(Q_BLOCK)
        rel = qpos[:, None] - kpos[None, :]
        qa_b = lax.dynamic_slice_in_dim(qa, t0, Q_BLOCK, axis=1)
        z = jnp.einsum('bqhd,bkhd->bhqk', qa_b, ka) * scale
        strict = rel > 0
        log_keep = jnp.where(strict, -jax.nn.softplus(z), 0.0)
        later = lax.cumsum(log_keep, axis=3, reverse=True) - log_keep
        w_sb = jnp.where(strict, jnp.exp(jax.nn.log_sigmoid(z) + later), 0.0)
        o_sb = jnp.einsum('bhqk,bkhd->bqhd', w_sb, va)
        qb_b = lax.dynamic_slice_in_dim(qb, t0, Q_BLOCK, axis=1)
        sc = jnp.einsum('bqhmd,bkhmd->bhmqk', qb_b, kb) * scale
        bias = table[t5_bucket(rel)]
        bias = bias[..., SB_HEADS:].reshape(Q_BLOCK, S, DIFF_HEADS, 2).transpose(2, 3, 0, 1)
        sc = jnp.where(rel >= 0, sc + bias, NEG)
        prob = jax.nn.softmax(sc, axis=-1)
        attn = prob[:, :, 0] - lam * prob[:, :, 1]
        o_d = jnp.einsum('bhqk,bkhd->bqhd', attn, vb)
        o_d = rmsnorm(o_d, subln_g, SUBLN_EPS) * (1.0 - lambda_init)
        return jnp.concatenate([o_sb.reshape(B, Q_BLOCK, SB_WIDTH),
                                o_d.reshape(B, Q_BLOCK, DIFF_WIDTH)], axis=-1)

    o = lax.map(block, jnp.arange(S // Q_BLOCK))
    o = o.transpose(1, 0, 2, 3).reshape(B, S, SB_WIDTH + DIFF_WIDTH).astype(h.dtype)
    return o @ w_out


def odd_mixer(h, w_in, w_out, t5_table):
    B, S, _ = h.shape
    f32 = jnp.float32
    q, k, v = jnp.split(h @ w_in, 3, axis=-1)
    q = q.reshape(B, S, DIL_HEADS, HEAD_DIM).astype(f32)
    k = k.reshape(B, S, DIL_HEADS, HEAD_DIM).astype(f32)
    v = v.reshape(B, S, DIL_HEADS, HEAD_DIM).astype(f32)
    pad = ((0, 0), (W_MAX, 0), (0, 0), (0, 0))
    kp = jnp.pad(k, pad)
    vp = jnp.pad(v, pad)
    scale = HEAD_DIM ** -0.5
    table = t5_table.astype(f32)

    def block(i):
        t0 = i * Q_BLOCK
        q_b = lax.dynamic_slice_in_dim(q, t0, Q_BLOCK, axis=1)
        outs, lses = [], []
        for (w, r) in DIL_BRANCHES:
            L = w + Q_BLOCK
            nq, nk = Q_BLOCK // r, L // r
            k_s = lax.dynamic_slice_in_dim(kp, t0 + W_MAX - w, L, axis=1).reshape(B, nk, r, DIL_HEADS, HEAD_DIM)
            v_s = lax.dynamic_slice_in_dim(vp, t0 + W_MAX - w, L, axis=1).reshape(B, nk, r, DIL_HEADS, HEAD_DIM)
            q_s = q_b.reshape(B, nq, r, DIL_HEADS, HEAD_DIM)
            sc = jnp.einsum('bqchd,bkchd->bhcqk', q_s, k_s) * scale
            dist = w + (jnp.arange(nq)[:, None] - jnp.arange(nk)[None, :]) * r
            keypos = t0 - w + jnp.arange(nk)[None, :] * r + jnp.arange(r)[:, None]
            valid = ((dist >= 0) & (dist <= w))[None] & (keypos >= 0)[:, None, :]
            bias = table[t5_bucket(dist)].transpose(2, 0, 1)[:, None]
            sc = jnp.where(valid, sc + bias, NEG)
            lse = jax.nn.logsumexp(sc, axis=-1, keepdims=True)
            pr = jnp.exp(sc - lse)
            o = jnp.einsum('bhcqk,bkchd->bqchd', pr, v_s).reshape(B, Q_BLOCK, DIL_HEADS, HEAD_DIM)
            lse = lse[..., 0].transpose(0, 3, 2, 1).reshape(B, Q_BLOCK, DIL_HEADS)
            outs.append(o)
            lses.append(lse)
        wts = jax.nn.softmax(jnp.stack(lses, axis=0), axis=0)
        o = jnp.sum(wts[..., None] * jnp.stack(outs, axis=0), axis=0)
        return o.reshape(B, Q_BLOCK, DIL_HEADS * HEAD_DIM)

    o = lax.map(block, jnp.arange(S // Q_BLOCK))
    o = o.transpose(1, 0, 2, 3).reshape(B, S, DIL_HEADS * HEAD_DIM).astype(h.dtype)
    return o @ w_out


def setup_inputs(seed: int = 0) -> dict:
    key = jax.random.key(seed)
    ks = jax.random.split(key, 24)
    nrm = jax.random.normal
    f32 = jnp.float32
    D = D_MODEL
    return {
        "x": nrm(ks[0], (BATCH, SEQ, D), f32),
        "p": nrm(ks[1], (DEPTH, BATCH, SEQ, PLE_DIM), f32),
        "t5_table": 0.5 * nrm(ks[2], (NUM_BUCKETS, N_SLOTS), f32),
        "w_in_even": nrm(ks[3], (N_EVEN, D, EVEN_IN), f32) * D ** -0.5,
        "w_out_even": nrm(ks[4], (N_EVEN, SB_WIDTH + DIFF_WIDTH, D), f32) * (SB_WIDTH + DIFF_WIDTH) ** -0.5,
        "lambda_q1": 0.1 * nrm(ks[5], (N_EVEN, HEAD_DIM), f32),
        "lambda_k1": 0.1 * nrm(ks[6], (N_EVEN, HEAD_DIM), f32),
        "lambda_q2": 0.1 * nrm(ks[7], (N_EVEN, HEAD_DIM), f32),
        "lambda_k2": 0.1 * nrm(ks[8], (N_EVEN, HEAD_DIM), f32),
        "subln_g": 1.0 + 0.02 * nrm(ks[9], (N_EVEN, DIFF_VDIM), f32),
        "w_in_odd": nrm(ks[10], (N_ODD, D, ODD_IN), f32) * D ** -0.5,
        "w_out_odd": nrm(ks[11], (N_ODD, DIL_HEADS * HEAD_DIM, D), f32) * (DIL_HEADS * HEAD_DIM) ** -0.5,
        "norm_mix_g": 1.0 + 0.02 * nrm(ks[12], (DEPTH, D), f32),
        "norm_mlp_g": 1.0 + 0.02 * nrm(ks[13], (DEPTH, D), f32),
        "w_mlp_up": nrm(ks[14], (DEPTH, D, D_FF), f32) * D ** -0.5,
        "w_mlp_down": nrm(ks[15], (DEPTH, D_FF, D), f32) * D_FF ** -0.5,
        "norm_ple_g": 1.0 + 0.02 * nrm(ks[16], (DEPTH, D), f32),
        "w_ple_gate": nrm(ks[17], (DEPTH, D, D), f32) * D ** -0.5,
        "w_ple_proj": nrm(ks[18], (DEPTH, PLE_DIM, D), f32) * PLE_DIM ** -0.5,
        "final_norm_g": 1.0 + 0.02 * nrm(ks[19], (D,), f32),
    }


def reference(x, p, t5_table, w_in_even, w_out_even, lambda_q1, lambda_k1, lambda_q2, lambda_k2,
              subln_g, w_in_odd, w_out_odd, norm_mix_g, norm_mlp_g, w_mlp_up, w_mlp_down,
              norm_ple_g, w_ple_gate, w_ple_proj, final_norm_g):
    h = x
    for i in range(DEPTH):
        hn = rmsnorm(h, norm_mix_g[i])
        if i % 2 == 0:
            e = i // 2
            h = h + even_mixer(hn, w_in_even[e], w_out_even[e], lambda_q1[e], lambda_k1[e],
                               lambda_q2[e], lambda_k2[e], subln_g[e], t5_table, i)
        else:
            o = i // 2
            h = h + odd_mixer(hn, w_in_odd[o], w_out_odd[o], t5_table)
        hn = rmsnorm(h, norm_mlp_g[i])
        u = jnp.square(jax.nn.relu(hn @ w_mlp_up[i]))
        h = h + u @ w_mlp_down[i]
        gate = jax.nn.sigmoid(rmsnorm(h, norm_ple_g[i]) @ w_ple_gate[i])
        h = h + (p[i] @ w_ple_proj[i]) * gate
    return rmsnorm(h, final_norm_g)
```

```python
import math
import numpy as np
import ml_dtypes
import concourse.bass as bass
import concourse.mybir as mybir
from concourse.bass_utils import run_bass_kernel_spmd

F32 = mybir.dt.float32
BF16 = mybir.dt.bfloat16
AF = mybir.ActivationFunctionType
ALU = mybir.AluOpType

S = 8192
D = 1024
NTT = S // 512
NEG = -1e30
LAMBDA_INIT0 = 0.8 - 0.6 * math.exp(-0.3 * 0)


class Res:
    __slots__ = ("name", "w", "r", "dsem", "dcnt")

    def __init__(self, name):
        self.name = name
        self.w = None
        self.r = {}
        self.dsem = None
        self.dcnt = 0


class Eng:
    def __init__(self, nc, eng, name, is_pe=False):
        self.eng = eng
        self.name = name
        self.sem = nc.alloc_semaphore("sem_" + name)
        self.cnt = 0
        self.seen = {}
        self.is_pe = is_pe


class Tracker:
    def __init__(self, nc):
        self.nc = nc
        self.pe = Eng(nc, nc.tensor, "pe", is_pe=True)
        self.act = Eng(nc, nc.scalar, "act")
        self.dve = Eng(nc, nc.vector, "dve")
        self.pool = Eng(nc, nc.gpsimd, "pool")
        self.sp = Eng(nc, nc.sync, "sp")
        self.engs = [self.pe, self.act, self.dve, self.pool, self.sp]
        self.nd = 0
        self.free_dsems = []
        self.live = []

    def res(self, name):
        return Res(name)

    def give_dsem(self, r):
        if r.dsem is None:
            if self.free_dsems:
                r.dsem, r.dcnt = self.free_dsems.pop()
            else:
                self.nd += 1
                r.dsem = self.nc.alloc_semaphore("ds%d" % self.nd)
                r.dcnt = 0
            self.live.append(r)

    def _collect(self, E, reads, writes):
        need = {}

        def add(t, raw):
            if t is None:
                return
            sem, val, is_dma, owner = t
            if sem is E.sem:
                if E.is_pe or not raw:
                    return
            if is_dma:
                val = 16 * owner.dcnt
            k = id(sem)
            if k not in need or need[k][1] < val:
                need[k] = (sem, val)

        for r in reads:
            if r is not None:
                add(r.w, True)
        for w in writes:
            if w is None:
                continue
            add(w.w, False)
            for t in w.r.values():
                add(t, False)
        for k, (sem, val) in need.items():
            if E.seen.get(k, 0) < val:
                E.eng.wait_ge(sem, val)
                E.seen[k] = val

    def _record(self, tk, reads, writes):
        k = id(tk[0])
        reads = [r for r in reads if r is not None]
        writes = [w for w in writes if w is not None]
        for r in reads:
            old = r.r.get(k)
            if old is None or old[1] < tk[1]:
                r.r[k] = tk
        for w in writes:
            w.w = tk
            w.r = {}

    def op(self, E, fn, reads=(), writes=()):
        self._collect(E, reads, writes)
        ins = fn(E.eng)
        E.cnt += 1
        ins.then_inc(E.sem, 1)
        self._record((E.sem, E.cnt, False, None), reads, writes)

    def dma(self, Q, out, in_, reads, writes, dres):
        self.give_dsem(dres)
        self._collect(Q, reads, writes)
        ins = Q.eng.dma_start(out=out, in_=in_)
        dres.dcnt += 1
        ins.then_inc(dres.dsem, 16)
        self._record((dres.dsem, 16 * dres.dcnt, True, dres), reads, writes)

    def barrier(self, release=True):
        for E in self.engs:
            for O in self.engs:
                if O is E or O.cnt == 0:
                    continue
                k = id(O.sem)
                if E.seen.get(k, 0) < O.cnt:
                    E.eng.wait_ge(O.sem, O.cnt)
                    E.seen[k] = O.cnt
            for r in self.live:
                k = id(r.dsem)
                v = 16 * r.dcnt
                if v > 0 and E.seen.get(k, 0) < v:
                    E.eng.wait_ge(r.dsem, v)
                    E.seen[k] = v
        if release:
            for r in self.live:
                self.free_dsems.append((r.dsem, r.dcnt))
                r.dsem = None
            self.live = []


class Rot:
    def __init__(self, items):
        self.items = items
        self.i = 0

    def next(self):
        it = self.items[self.i % len(self.items)]
        self.i += 1
        return it


class Cx:
    pass


def t5_bucket_np(dist):
    n = np.maximum(dist, 0)
    nf = np.maximum(n, 1).astype(np.float32)
    large = 16 + (np.log(nf / np.float32(16)) / np.float32(math.log(128 / 16)) * np.float32(16)).astype(np.int32)
    large = np.minimum(large, 31)
    return np.where(n < 16, n, large)


def host_consts(t5_table):
    bf = ml_dtypes.bfloat16
    c = {}
    j = np.arange(128)
    cst = np.zeros((128, 4, 128), np.float32)
    cst[:, 0, :] = np.eye(128)
    cst[:, 1, :] = -(j[:, None] >= j[None, :]).astype(np.float32)
    cst[:, 2, :] = -1.0
    cst[:, 3, :] = 1.0
    c["cst"] = cst.astype(bf)
    c["onesf"] = np.ones((128, 128), np.float32)
    r = np.arange(128)[:, None]
    q = np.arange(512)[None, :]
    mA = np.zeros((128, 4, 512), np.float32)
    for i in range(4):
        mA[:, i, :] = np.where(128 * i + r < q, 0.0, NEG)
    c["maskA"] = mA.astype(bf)
    tab = np.asarray(t5_table, np.float32)
    bB = np.zeros((8, 128, 5, 512), np.float32)
    for ib in range(5):
        i = ib - 1
        dist = q - (128 * i + r)
        bk = t5_bucket_np(dist)
        for sl in range(8):
            g = tab[bk, 8 + sl]
            bB[sl, :, ib, :] = np.where(dist >= 0, g, np.float32(NEG))
    c["biasB"] = bB
    bC = np.zeros((16, 128, 3, 2, 128), np.float32)
    s_ = np.arange(128)[:, None]
    t_ = np.arange(128)[None, :]
    for ri, rr in enumerate((1, 4, 16)):
        dd = rr * (t_ - s_)
        dp = rr * (128 + t_ - s_)
        for hh in range(16):
            bC[hh, :, ri, 0, :] = np.where(dd >= 0, tab[t5_bucket_np(dd), hh], np.float32(NEG))
            bC[hh, :, ri, 1, :] = np.where(dp <= 128 * rr, tab[t5_bucket_np(dp), hh], np.float32(NEG))
    c["biasC"] = bC
    return c


def build_program(nphase=7, dbg=False):
    nc = bass.Bass("TRN2", target_bir_lowering=False)
    tk = Tracker(nc)
    cx = Cx()
    cx.nc, cx.tk = nc, tk

    def din(name, shape, dt=F32):
        return nc.dram_tensor(name, list(shape), dt, kind="ExternalInput").ap()

    def dscr(name, shape, dt):
        return nc.dram_tensor(name, list(shape), dt, kind=("ExternalOutput" if dbg else "Internal")).ap()

    I = {}
    I["x"] = din("x", [S, D])
    I["p"] = din("p", [2, S, 256])
    I["t5_table"] = din("t5_table", [32, 16])
    I["w_in_even"] = din("w_in_even", [D, 3072])
    I["w_in_odd"] = din("w_in_odd", [D, 3072])
    I["w_out"] = din("w_out", [2, D, D])
    I["lam"] = din("lam", [4, 64])
    I["subln_g"] = din("subln_g", [128, 1])
    I["norm_mix_g"] = din("norm_mix_g", [2, D])
    I["norm_mlp_g"] = din("norm_mlp_g", [2, D])
    I["norm_ple_g"] = din("norm_ple_g", [2, D])
    I["final_norm_g"] = din("final_norm_g", [1, D])
    I["w_mlp_up"] = din("w_mlp_up", [2, D, 4096])
    I["w_mlp_down"] = din("w_mlp_down", [2, 4096, D])
    I["w_ple_gate"] = din("w_ple_gate", [2, D, D])
    I["w_ple_proj"] = din("w_ple_proj", [2, 256, D])
    I["cst"] = din("cst", [128, 4, 128], BF16)
    I["onesf"] = din("onesf", [128, 128])
    I["maskA"] = din("maskA", [128, 4, 512], BF16)
    I["biasB"] = din("biasB", [8, 128, 5, 512])
    I["biasC"] = din("biasC", [16, 128, 3, 2, 128])
    out = nc.dram_tensor("out", [S, D], F32, kind="ExternalOutput").ap()
    cx.I = I

    Wb = {}
    for l in range(2):
        Wb["out", l] = dscr("wb_out%d" % l, [D, D], BF16)
        Wb["up", l] = dscr("wb_up%d" % l, [D, 4096], BF16)
        Wb["down", l] = dscr("wb_down%d" % l, [4096, D], BF16)
        Wb["gate", l] = dscr("wb_gate%d" % l, [D, D], BF16)
        Wb["ple", l] = dscr("wb_ple%d" % l, [256, D], BF16)
    cx.Wb = Wb
    fm = [dscr("fm%d" % l, [2048, S], BF16) for l in range(2)]
    tm = [dscr("tm%d" % l, [S, D], BF16) for l in range(2)]
    oT = [dscr("oT%d" % l, [D, S], BF16) for l in range(2)]
    h1 = dscr("h1", [S, D], F32)
    R_fm = [None, None]
    R_tm = [None, None]
    R_oT = [None, None]
    R_h1 = None
    R_out = None
    R_wb = None
    R_wbd = tk.res("wb")

    srcs = {"out": I["w_out"], "up": I["w_mlp_up"], "down": I["w_mlp_down"], "gate": I["w_ple_gate"], "ple": I["w_ple_proj"]}
    win1_b = dscr("wb_in1", [D, 3072], BF16)

    def prologue():
        for r0 in range(0, D, 128):
            tk.dma(tk.pool, win1_b[r0:r0 + 128, :], I["w_in_odd"][r0:r0 + 128, :], [], [], R_wbd)
        for l in range(2):
            for nm in ("out", "up", "down", "gate", "ple"):
                src = srcs[nm][l]
                dst = Wb[nm, l]
                rows = src.shape[0]
                for r0 in range(0, rows, 128):
                    tk.dma(tk.pool, dst[r0:r0 + 128, :], src[r0:r0 + 128, :], [], [], R_wbd)

    phase_proj(cx, 0, I["x"], None, I["norm_mix_g"][0:1, :], I["w_in_even"], True, prologue, fm[0], tm[0], R_fm[0], R_tm[0],
               [(128 * i, 128 * i, 0.125) for i in range(4)] + [(512 + 128 * i, 512 + 128 * i, 1.0) for i in range(4)]
               + [(1536 + 128 * i, 1024 + 128 * i, 0.125) for i in range(4)] + [(2048 + 128 * i, 1536 + 128 * i, 1.0) for i in range(4)],
               [(1024, 0), (2560, 512)])
    tk.barrier()
    if nphase >= 2:
        phase_mixA(cx, fm[0], tm[0], oT[0], R_fm[0], R_tm[0], R_oT[0])
        tk.barrier()
    if nphase >= 3:
        phase_mixB(cx, fm[0], tm[0], oT[0], R_fm[0], R_tm[0], R_oT[0])
        tk.barrier()
    if nphase >= 4:
        phase_tail(cx, 0, I["x"], None, oT[0], R_oT[0], h1, R_h1, R_wb, final=False)
        tk.barrier()
    if nphase >= 5:
        phase_proj(cx, 1, h1, R_h1, I["norm_mix_g"][1:2, :], win1_b, False, None, fm[1], tm[1], R_fm[1], R_tm[1],
                   [(128 * i, 128 * i, 0.125) for i in range(8)] + [(1024 + 128 * i, 1024 + 128 * i, 1.0) for i in range(8)],
                   [(2048, 0), (2560, 512)])
        tk.barrier()
    if nphase >= 6:
        phase_mixC(cx, fm[1], tm[1], oT[1], R_fm[1], R_tm[1], R_oT[1])
        tk.barrier()
    if nphase >= 7:
        phase_tail(cx, 1, h1, R_h1, oT[1], R_oT[1], out, R_out, R_wb, final=True)
    tk.barrier(release=False)
    return nc


class Alloc:
    def __init__(self, cx, tag):
        import contextlib
        self.cx = cx
        self.tag = tag
        self.stk = contextlib.ExitStack()
        self.n = 0

    def sb(self, shape, dt, name=None):
        self.n += 1
        h = self.stk.enter_context(self.cx.nc.sbuf_tensor("%s_s%d" % (self.tag, self.n), list(shape), dt))
        return h, self.cx.tk.res("%s_s%d" % (self.tag, self.n))

    def ps(self, shape, dt):
        self.n += 1
        h = self.stk.enter_context(self.cx.nc.psum_tensor("%s_p%d" % (self.tag, self.n), list(shape), dt))
        return h, self.cx.tk.res("%s_p%d" % (self.tag, self.n))

    def close(self):
        self.stk.close()


def rms_stats(cx, xt, Rx, st, Rst, junk, Rjunk, nj, eps, width):
    tk = cx.tk
    for j in range(nj):
        tk.op(tk.act, lambda e, j=j: e.activation(out=junk[:, 0:width], in_=xt[:, j, :], func=AF.Square,
                                                  accum_out=st[:, j:j + 1]), [Rx], [Rjunk, Rst])
    tk.op(tk.act, lambda e: e.activation(out=st[:, nj:2 * nj], in_=st[:, 0:nj], func=AF.Sqrt,
                                         scale=1.0 / width, bias=eps), [Rst], [Rst])
    tk.op(tk.dve, lambda e: e.reciprocal(out=st[:, nj:2 * nj], in_=st[:, nj:2 * nj]), [Rst], [Rst])


def norm_transpose(cx, xt, Rx, gt, Rg, st, Rst, hn, Rhn, hnT, RhnT, psTs, ident, Rid, flip):
    tk = cx.tk
    for j in range(4):
        tk.op(tk.dve, lambda e, j=j: e.scalar_tensor_tensor(out=hn[:, j, :], in0=xt[:, j, :], scalar=st[:, 4 + j:5 + j],
                                                            in1=gt[:], op0=ALU.mult, op1=ALU.mult),
              [Rx, Rst, Rg], [Rhn])
    for j in range(4):
        psT, RpsT = psTs.next()
        for c in range(8):
            tk.op(tk.pe, lambda e, j=j, c=c, psT=psT: e.transpose(out=psT[:, c * 128:(c + 1) * 128],
                                                                 in_=hn[:, j, c * 128:(c + 1) * 128], identity=ident[:]),
                  [Rhn, Rid], [RpsT])
        src = psT[:].rearrange("p (c t) -> p c t", c=8)
        dst = hnT[:, :, j * 128:(j + 1) * 128]
        if (j + flip) % 2 == 0:
            tk.op(tk.act, lambda e, src=src, dst=dst: e.copy(out=dst, in_=src), [RpsT], [RhnT])
        else:
            tk.op(tk.dve, lambda e, src=src, dst=dst: e.tensor_copy(out=dst, in_=src), [RpsT], [RhnT])


def warm(cx, bank, Rbank, lhsT, RlhsT, rhs, Rrhs, n):
    tk = cx.tk
    for i in range(n):
        tk.op(tk.pe, lambda e: e.matmul(bank[:], lhsT=lhsT, rhs=rhs, start=True, stop=True), [RlhsT, Rrhs], [Rbank])


def phase_proj(cx, L, src, Rsrc, gvec, win, win_cast, after_setup, fm, tm, Rfm, Rtm, fmchunks, tmgroups):
    nc, tk, I = cx.nc, cx.tk, cx.I
    A = Alloc(cx, "pj%d" % L)
    wsb, Rw = A.sb([128, 8, 3072], BF16)
    gt, Rg = A.sb([128, D], F32)
    ident, Rid = A.sb([128, 128], BF16)
    xts = [A.sb([128, 4, D], F32) for _ in range(2)]
    junk, Rjunk = A.sb([128, D], BF16)
    sts = [A.sb([128, 8], F32) for _ in range(2)]
    hns = [A.sb([128, 4, D], BF16) for _ in range(2)]
    hnTs = [A.sb([128, 8, 512], BF16) for _ in range(2)]
    stg = Rot([A.sb([128, 512], BF16) for _ in range(6)])
    psTs = Rot([A.ps([128, 1024], BF16) for _ in range(2)])
    banks = Rot([A.ps([128, 512], F32) for _ in range(6)])

    def load(tt):
        xt, Rx = xts[tt % 2]
        tk.dma(tk.sp, xt[:], src[tt * 512:(tt + 1) * 512, :].rearrange("(j p) d -> p j d", p=128),
               [Rsrc] if Rsrc is not None else [], [Rx], Rx)

    tk.dma(tk.sp, gt[:], gvec.broadcast_to([128, D]), [], [Rg], Rg)
    tk.dma(tk.sp, ident[:], I["cst"][:, 0, :], [], [Rid], Rid)
    load(0)
    if win_cast:
        wst = [A.sb([128, 3072], F32) for _ in range(2)]
        Rws = [tk.res("wsbc%d" % c) for c in range(8)]
        for c in range(8):
            ws_, Rs_ = wst[c % 2]
            tk.dma(tk.sp, ws_[:], win[c * 128:(c + 1) * 128, :], [], [Rs_], Rs_)
            if c % 2 == 0:
                tk.op(tk.dve, lambda e, c=c, ws_=ws_: e.tensor_copy(out=wsb[:, c, :], in_=ws_[:]), [Rs_], [Rws[c]])
            else:
                tk.op(tk.act, lambda e, c=c, ws_=ws_: e.copy(out=wsb[:, c, :], in_=ws_[:]), [Rs_], [Rws[c]])
    else:
        Rws = [Rw] * 8
        for c in range(8):
            tk.dma(tk.sp, wsb[:, c, :], win[c * 128:(c + 1) * 128, :], [], [Rw], Rw)
    if after_setup is not None:
        after_setup()

    def stageN(tt):
        xt, Rx = xts[tt % 2]
        st, Rst = sts[tt % 2]
        hn, Rhn = hns[tt % 2]
        rms_stats(cx, xt, Rx, st, Rst, junk, Rjunk, 4, 1e-6, D)
        for j in range(4):
            tk.op(tk.dve, lambda e, j=j: e.scalar_tensor_tensor(out=hn[:, j, :], in0=xt[:, j, :], scalar=st[:, 4 + j:5 + j],
                                                                in1=gt[:], op0=ALU.mult, op1=ALU.mult), [Rx, Rst, Rg], [Rhn])

    def stageT(tt):
        hn, Rhn = hns[tt % 2]
        hnT, RhnT = hnTs[tt % 2]
        for j in range(4):
            psT, RpsT = psTs.next()
            for c in range(8):
                tk.op(tk.pe, lambda e, j=j, c=c, psT=psT: e.transpose(out=psT[:, c * 128:(c + 1) * 128],
                                                                     in_=hn[:, j, c * 128:(c + 1) * 128], identity=ident[:]),
                      [Rhn, Rid], [RpsT])
            srcp = psT[:].rearrange("p (c t) -> p c t", c=8)
            dst = hnT[:, :, j * 128:(j + 1) * 128]
            if j % 2 == 0:
                tk.op(tk.act, lambda e, srcp=srcp, dst=dst: e.copy(out=dst, in_=srcp), [RpsT], [RhnT])
            else:
                tk.op(tk.dve, lambda e, srcp=srcp, dst=dst: e.tensor_copy(out=dst, in_=srcp), [RpsT], [RhnT])

    ev = [0]

    def stageM(tt):
        t0 = tt * 512
        hnT, RhnT = hnTs[tt % 2]
        for (wc, fr, sc) in fmchunks:
            bk, Rb = banks.next()
            for c in range(8):
                tk.op(tk.pe, lambda e, c=c, bk=bk, wc=wc: e.matmul(bk[:], lhsT=wsb[:, c, wc:wc + 128], rhs=hnT[:, c, :],
                                                                    start=(c == 0), stop=(c == 7)), [Rws[c], RhnT], [Rb])
            sg, Rsg = stg.next()
            ev[0] += 1
            if ev[0] % 2 == 0:
                tk.op(tk.act, lambda e, sg=sg, bk=bk, sc=sc: e.activation(out=sg[:], in_=bk[:], func=AF.Copy, scale=sc), [Rb], [Rsg])
            else:
                tk.op(tk.dve, lambda e, sg=sg, bk=bk, sc=sc: e.tensor_scalar_mul(out=sg[:], in0=bk[:], scalar1=sc), [Rb], [Rsg])
            tk.dma(tk.pool, fm[fr:fr + 128, t0:t0 + 512], sg[:], [Rsg], [Rfm], Rsg)
        for j in range(4):
            for (wc, tc) in tmgroups:
                bk, Rb = banks.next()
                for c in range(8):
                    tk.op(tk.pe, lambda e, c=c, bk=bk, wc=wc, j=j: e.matmul(bk[:], lhsT=hnT[:, c, j * 128:(j + 1) * 128],
                                                                            rhs=wsb[:, c, wc:wc + 512], start=(c == 0), stop=(c == 7)),
                          [Rws[c], RhnT], [Rb])
                sg, Rsg = stg.next()
                ev[0] += 1
                if ev[0] % 2 == 0:
                    tk.op(tk.act, lambda e, sg=sg, bk=bk: e.copy(out=sg[:], in_=bk[:]), [Rb], [Rsg])
                else:
                    tk.op(tk.dve, lambda e, sg=sg, bk=bk: e.tensor_copy(out=sg[:], in_=bk[:]), [Rb], [Rsg])
                tk.dma(tk.pool, tm[t0 + j * 128:t0 + (j + 1) * 128, tc:tc + 512], sg[:], [Rsg], [Rtm], Rsg)

    stageN(0)
    stageT(0)
    for tt in range(NTT):
        if tt + 1 < NTT:
            load(tt + 1)
            stageN(tt + 1)
        stageM(tt)
        if tt + 1 < NTT:
            stageT(tt + 1)
    tk.barrier()
    A.close()


def phase_mixA(cx, fm, tm, oT, Rfm, Rtm, RoT):
    nc, tk, I = cx.nc, cx.tk, cx.I
    A = Alloc(cx, "mA")
    va, Rva = A.sb([128, 64, 512], BF16)
    qTs = [A.sb([128, S], BF16) for _ in range(2)]
    kTs = [A.sb([128, S], BF16) for _ in range(2)]
    for (t_, R_) in qTs + kTs:
        tk.op(tk.pool, lambda e, t_=t_: e.memset(t_[64:128, :], 0.0), [], [R_])

    def loadqk(h):
        tk.dma(tk.sp, qTs[h % 2][0][0:64, :], fm[64 * h:64 * h + 64, :], [Rfm], [qTs[h % 2][1]], qTs[h % 2][1])
        tk.dma(tk.sp, kTs[h % 2][0][0:64, :], fm[512 + 64 * h:512 + 64 * h + 64, :], [Rfm], [kTs[h % 2][1]], kTs[h % 2][1])
    cst, Rcst = A.sb([128, 4, 128], BF16)
    mA, RmA = A.sb([128, 4, 512], BF16)
    es = [A.sb([128, 512], BF16) for _ in range(2)]
    sps = [A.sb([128, 512], BF16) for _ in range(2)]
    ws = [A.sb([128, 512], BF16) for _ in range(2)]
    accs = [A.sb([128, 512], BF16) for _ in range(2)]
    spD = {dg_: A.sb([128, 512], BF16) for dg_ in (1, 2, 3)}
    wD = {dg_: A.sb([128, 512], BF16) for dg_ in (1, 2, 3)}
    for dg_ in (1, 2, 3):
        tk.op(tk.pool, lambda e, dg_=dg_: e.memset(spD[dg_][0][:, 0:128 * dg_], 0.0), [], [spD[dg_][1]])
        tk.op(tk.pool, lambda e, dg_=dg_: e.memset(wD[dg_][0][:, 0:128 * dg_], 0.0), [], [wD[dg_][1]])

    def spbuf(i):
        dg_ = tiles[i]["dg"]
        return spD[dg_] if dg_ in (1, 2, 3) else sps[i % 2]

    def wbuf(i):
        dg_ = tiles[i]["dg"]
        return wD[dg_] if dg_ in (1, 2, 3) else ws[i % 2]

    def csl(i):
        dg_ = tiles[i]["dg"]
        return slice(128 * dg_, 512) if dg_ in (1, 2, 3) else slice(0, 512)
    osbs = [A.sb([128, 512], BF16) for _ in range(2)]
    Z1 = [A.ps([128, 512], F32) for _ in range(2)]
    Z2 = [A.ps([128, 512], F32) for _ in range(2)]
    Ob = [A.ps([128, 512], F32) for _ in range(2)]
    WB, RWB = A.ps([128, 512], F32)
    tk.dma(tk.sp, cst[:], I["cst"][:, :, :], [], [Rcst], Rcst)
    tk.dma(tk.sp, mA[:], I["maskA"][:, :, :], [], [RmA], RmA)
    tmv = tm.rearrange("(b p) c -> p b c", p=128)
    Rvas = [tk.res("va%d" % g) for g in range(8)]

    def load_v():
        for b0 in range(0, 64, 8):
            tk.dma(tk.sp, va[:, b0:b0 + 8, :], tmv[:, b0:b0 + 8, 0:512], [Rtm], [Rvas[b0 // 8]], Rvas[b0 // 8])
    ident = cst[:, 0, :]
    NTm = cst[:, 1, :]
    NOm = cst[:, 2, :]

    tiles = []
    for h in range(8):
        for qt in range(NTT):
            kbs = list(range(4 * qt + 3, -1, -1))
            for ki, kb in enumerate(kbs):
                tiles.append(dict(h=h, qt=qt, kb=kb, first=(ki == 0), last=(kb == 0),
                                  dg=(kb - 4 * qt if kb >= 4 * qt else None), newh=(qt == 0 and ki == 0)))
    n = len(tiles)
    for i, t in enumerate(tiles):
        t["ob"] = (t["h"] * NTT + t["qt"]) % 2

    def qk(t, Z, RZ, extra_stop):
        kb, t0 = t["kb"], t["qt"] * 512
        dg = t["dg"]
        qT, RqT = qTs[t["h"] % 2]
        kT, RkT = kTs[t["h"] % 2]
        tk.op(tk.pe, lambda e: e.matmul(Z[:], lhsT=kT[:, kb * 128:(kb + 1) * 128], rhs=qT[:, t0:t0 + 512],
                                        start=True, stop=(dg is None and extra_stop)), [RkT, RqT], [RZ])
        if dg is not None:
            tk.op(tk.pe, lambda e: e.matmul(Z[:], lhsT=ident, rhs=mA[:, dg, :], start=False, stop=extra_stop), [Rcst, RmA], [RZ])

    def s0(i):
        t = tiles[i]
        if t["newh"]:
            h = t["h"]
            if h == 0:
                loadqk(0)
                load_v()
            if h + 1 < 8:
                loadqk(h + 1)
            qT, RqT = qTs[h % 2]
            kT, RkT = kTs[h % 2]
            warm(cx, WB, RWB, kT[:, 0:128], RkT, qT[:, 0:512], RqT, 12)
        Z, RZ = Z1[i % 2]
        qk(t, Z, RZ, True)

    def s1a(i):
        Z, RZ = Z1[i % 2]
        ee, Re = es[i % 2]
        cs = csl(i)
        tk.op(tk.act, lambda e: e.activation(out=ee[:, cs], in_=Z[:, cs], func=AF.Exp), [RZ], [Re])

    def s1b(i):
        ee, Re = es[i % 2]
        sp, Rsp = spbuf(i)
        cs = csl(i)
        tk.op(tk.act, lambda e: e.activation(out=sp[:, cs], in_=ee[:, cs], func=AF.Ln, bias=1.0, scale=1.0), [Re], [Rsp])

    def s2(i):
        t = tiles[i]
        Z, RZ = Z2[i % 2]
        sp, Rsp = spbuf(i)
        qk(t, Z, RZ, False)
        tk.op(tk.pe, lambda e: e.matmul(Z[:], lhsT=NTm, rhs=sp[:], start=False, stop=t["first"]), [Rcst, Rsp], [RZ])
        ac, Rac = accs[i % 2]
        an, Ran = accs[(i + 1) % 2]
        if not t["first"]:
            tk.op(tk.pe, lambda e: e.matmul(Z[:], lhsT=NOm, rhs=ac[:], start=False, stop=True), [Rcst, Rac], [RZ])
        if not t["last"]:
            if t["first"]:
                tk.op(tk.dve, lambda e: e.tensor_copy(out=an[:], in_=sp[:]), [Rsp], [Ran])
            else:
                tk.op(tk.dve, lambda e: e.tensor_tensor(out=an[:], in0=ac[:], in1=sp[:], op=ALU.add), [Rac, Rsp], [Ran])

    def s3(i):
        Z, RZ = Z2[i % 2]
        w, Rw = wbuf(i)
        cs = csl(i)
        tk.op(tk.act, lambda e: e.activation(out=w[:, cs], in_=Z[:, cs], func=AF.Exp), [RZ], [Rw])

    def s4(i):
        t = tiles[i]
        w, Rw = wbuf(i)
        O, RO = Ob[t["ob"]]
        h, kb = t["h"], t["kb"]
        hp = 128 * (h // 2)
        rows = slice(64 * (h % 2), 64 * (h % 2) + 64)
        tk.op(tk.pe, lambda e: e.matmul(O[:, :], lhsT=va[:, kb, hp:hp + 128], rhs=w[:],
                                        start=t["first"], stop=t["last"]), [Rvas[kb // 8], Rw], [RO])
        if t["last"]:
            osb, Ros = osbs[t["ob"]]
            t0 = t["qt"] * 512
            tk.op(tk.dve, lambda e: e.tensor_copy(out=osb[rows, :], in_=O[rows, :]), [RO], [Ros])
            tk.dma(tk.pool, oT[64 * h:64 * h + 64, t0:t0 + 512], osb[rows, :], [Ros], [RoT], Ros)

    s0(0)
    for step in range(n + 3):
        if step + 1 < n:
            s0(step + 1)
        if step < n:
            s1a(step)
        if 0 <= step - 1 < n:
            s2(step - 1)
        if 0 <= step - 2 < n:
            s3(step - 2)
        if step < n:
            s1b(step)
        if 0 <= step - 3 < n:
            s4(step - 3)
    tk.barrier()
    A.close()


def phase_mixB(cx, fm, tm, oT, Rfm, Rtm, RoT):
    nc, tk, I = cx.nc, cx.tk, cx.I
    A = Alloc(cx, "mB")
    vb, Rvb = A.sb([128, 64, 512], BF16)
    qT, RqT = A.sb([128, 2, S], BF16)
    kT, RkT = A.sb([128, S], BF16)
    cst, Rcst = A.sb([128, 4, 128], BF16)
    onesf, Rof = A.sb([128, 128], F32)
    bias, Rbias = A.sb([128, 2, 5, 512], F32)
    tb31, Rtb = A.sb([128, 16], F32)
    lamt, Rlam = A.sb([128, 4, 64], F32)
    lsc, Rlsc = A.sb([128, 8], F32)
    gsc, Rgsc = A.sb([128, 1], F32)
    tmps = [A.sb([128, 512], F32) for _ in range(4)]
    Ps = [A.sb([128, 512], BF16) for _ in range(4)]
    cb = [A.sb([128, 512], F32) for _ in range(5)]
    osb, Ros = A.sb([128, 512], BF16)
    Zs = [A.ps([128, 512], F32) for _ in range(2)]
    U = [A.ps([128, 512], F32) for _ in range(2)]
    Lp = [A.ps([128, 512], F32) for _ in range(2)]
    SSQ, Rssq = A.ps([128, 512], F32)
    WB, RWB = A.ps([128, 512], F32)

    tk.op(tk.pool, lambda e: e.memset(qT[64:128, 0, :], 0.0), [], [RqT])
    tk.op(tk.pool, lambda e: e.memset(qT[0:64, 1, :], 0.0), [], [RqT])
    tk.dma(tk.sp, cst[:], I["cst"][:, :, :], [], [Rcst], Rcst)
    tk.dma(tk.sp, onesf[:], I["onesf"][:, :], [], [Rof], Rof)
    tk.dma(tk.sp, tb31[:], I["t5_table"][31:32, :].broadcast_to([128, 16]), [], [Rtb], Rtb)
    tk.dma(tk.sp, lamt[:], I["lam"].rearrange("(o a) d -> o a d", o=1).broadcast_to([128, 4, 64]), [], [Rlam], Rlam)
    tk.dma(tk.sp, gsc[:], I["subln_g"][:, :], [], [Rgsc], Rgsc)
    tmv = tm.rearrange("(b p) c -> p b c", p=128)
    Rvbs = [tk.res("vb%d" % g) for g in range(8)]

    def load_v():
        for b0 in range(0, 64, 8):
            tk.dma(tk.sp, vb[:, b0:b0 + 8, :], tmv[:, b0:b0 + 8, 512:1024], [Rtm], [Rvbs[b0 // 8]], Rvbs[b0 // 8])
    onesb = cst[:, 3, :]
    t1, Rt1 = cb[0]
    tk.op(tk.dve, lambda e: e.tensor_tensor(out=t1[:, 0:64], in0=lamt[:, 0, :], in1=lamt[:, 1, :], op=ALU.mult), [Rlam], [Rt1])
    tk.op(tk.dve, lambda e: e.tensor_tensor(out=t1[:, 64:128], in0=lamt[:, 2, :], in1=lamt[:, 3, :], op=ALU.mult), [Rlam], [Rt1])
    tk.op(tk.dve, lambda e: e.reduce_sum(out=lsc[:, 0:1], in_=t1[:, 0:64], axis=mybir.AxisListType.X), [Rt1], [Rlsc])
    tk.op(tk.dve, lambda e: e.reduce_sum(out=lsc[:, 1:2], in_=t1[:, 64:128], axis=mybir.AxisListType.X), [Rt1], [Rlsc])
    tk.op(tk.act, lambda e: e.activation(out=lsc[:, 2:4], in_=lsc[:, 0:2], func=AF.Exp), [Rlsc], [Rlsc])
    tk.op(tk.dve, lambda e: e.scalar_tensor_tensor(out=lsc[:, 4:5], in0=lsc[:, 3:4], scalar=-LAMBDA_INIT0, in1=lsc[:, 2:3],
                                                   op0=ALU.add, op1=ALU.subtract), [Rlsc], [Rlsc])
    tk.op(tk.dve, lambda e: e.tensor_scalar_mul(out=gsc[:], in0=gsc[:], scalar1=1.0 - LAMBDA_INIT0), [Rgsc], [Rgsc])

    tiles = []
    for h in range(4):
        for qt in range(NTT):
            for m in range(2):
                near = [kb for kb in range(4 * qt + 3, -1, -1) if kb - 4 * qt + 1 >= 0]
                far = [kb for kb in range(4 * qt + 3, -1, -1) if kb - 4 * qt + 1 < 0]
                tot = len(near) + len(far)
                pos = set(int((j + 0.5) * tot / len(near)) for j in range(len(near)))
                kbs = []
                ni, fi = 0, 0
                for k in range(tot):
                    if (k in pos and ni < len(near)) or fi >= len(far):
                        kbs.append(near[ni]); ni += 1
                    else:
                        kbs.append(far[fi]); fi += 1
                for ki, kb in enumerate(kbs):
                    ib = kb - 4 * qt + 1
                    tiles.append(dict(h=h, qt=qt, m=m, kb=kb, first=(ki == 0), last=(ki == tot - 1),
                                      ib=(ib if ib >= 0 else None), newh=(qt == 0 and m == 0 and ki == 0)))
    n = len(tiles)
    dqueue = []
    pend_reads = [0, 0]

    def dq(fn, mtag=None):
        dqueue.append((fn, mtag))
        if mtag is not None:
            pend_reads[mtag] += 1

    def dpop():
        fn, mtag = dqueue.pop(0)
        if mtag is not None:
            pend_reads[mtag] -= 1
        fn()

    def s0(i):
        t = tiles[i]
        h, m, kb, t0 = t["h"], t["m"], t["kb"], t["qt"] * 512
        if t["newh"]:
            tk.dma(tk.sp, qT[0:64, 0, :], fm[1024 + 128 * h:1024 + 128 * h + 64, :], [Rfm], [RqT], RqT)
            tk.dma(tk.sp, qT[64:128, 1, :], fm[1024 + 128 * h + 64:1024 + 128 * h + 128, :], [Rfm], [RqT], RqT)
            tk.dma(tk.sp, kT[:], fm[1536 + 128 * h:1536 + 128 * h + 128, :], [Rfm], [RkT], RkT)
            for mm in range(2):
                tk.dma(tk.sp, bias[:, mm, :, :], I["biasB"][2 * h + mm], [], [Rbias], Rbias)
            if h == 0:
                load_v()
            for mm in range(2):
                sl = 8 + 2 * h + mm
                for ib_ in range(5):
                    tk.op(tk.dve, lambda e, mm=mm, ib_=ib_, sl=sl: e.tensor_scalar(
                        out=bias[:, mm, ib_, :], in0=bias[:, mm, ib_, :], scalar1=tb31[:, sl:sl + 1], scalar2=None, op0=ALU.subtract),
                        [Rbias, Rtb], [Rbias])
            warm(cx, WB, RWB, kT[:, 0:128], RkT, qT[:, 0, 0:512], RqT, 12)
        Z, RZ = Zs[i % 2]
        tk.op(tk.pe, lambda e: e.matmul(Z[:], lhsT=kT[:, kb * 128:(kb + 1) * 128], rhs=qT[:, m, t0:t0 + 512],
                                        start=True, stop=True), [RkT, RqT], [RZ])

    def s1(i):
        t = tiles[i]
        h, m = t["h"], t["m"]
        Z, RZ = Zs[i % 2]
        P, RP = Ps[i % 4]
        if t["ib"] is not None:
            tmp, Rtmp = tmps[i % 4]
            tk.op(tk.dve, lambda e: e.tensor_tensor(out=tmp[:], in0=Z[:], in1=bias[:, m, t["ib"], :], op=ALU.add), [RZ, Rbias], [Rtmp])
            tk.op(tk.act, lambda e: e.activation(out=P[:], in_=tmp[:], func=AF.Exp), [Rtmp], [RP])
        else:
            tk.op(tk.act, lambda e: e.activation(out=P[:], in_=Z[:], func=AF.Exp), [RZ], [RP])

    def s2(i):
        t = tiles[i]
        h, m, kb = t["h"], t["m"], t["kb"]
        P, RP = Ps[i % 4]
        Um, RU = U[m]
        Lm, RL = Lp[m]
        if t["first"]:
            while pend_reads[m] > 0:
                dpop()
            pass
        tk.op(tk.pe, lambda e: e.matmul(Lm[:], lhsT=onesb, rhs=P[:], start=t["first"], stop=t["last"]), [Rcst, RP], [RL])
        tk.op(tk.pe, lambda e: e.matmul(Um[:], lhsT=vb[:, kb, 128 * h:128 * h + 128], rhs=P[:], start=t["first"], stop=t["last"]), [Rvbs[kb // 8], RP], [RU])
        if t["last"]:
            combineM(m)
            if m == 1:
                combineA(t)
                combineB(t)

    def combineM(m):
        (r0, Rr0), (o0, Ro0), (r1, Rr1), (o1, Ro1), (od, Rod) = cb
        rr, Rrr = (r0, Rr0) if m == 0 else (r1, Rr1)
        oo, Roo = (o0, Ro0) if m == 0 else (o1, Ro1)
        dq(lambda: tk.op(tk.act, lambda e: e.activation(out=rr[:], in_=Lp[m][0][:], func=AF.Ln), [Lp[m][1]], [Rrr]), m)
        dq(lambda: tk.op(tk.dve, lambda e: e.tensor_copy(out=oo[:], in_=U[m][0][:]), [U[m][1]], [Roo]), m)
        dq(lambda: tk.op(tk.act, lambda e: e.activation(out=rr[:], in_=rr[:], func=AF.Exp, scale=-1.0), [Rrr], [Rrr]))
        dq(lambda: tk.op(tk.dve, lambda e: e.tensor_tensor(out=oo[:], in0=oo[:], in1=rr[:], op=ALU.mult), [Roo, Rrr], [Roo]))

    def combineA(t):
        (r0, Rr0), (o0, Ro0), (r1, Rr1), (o1, Ro1), (od, Rod) = cb
        dq(lambda: tk.op(tk.dve, lambda e: e.scalar_tensor_tensor(out=od[:], in0=o1[:], scalar=lsc[:, 4:5], in1=o0[:], op0=ALU.mult, op1=ALU.add),
              [Ro1, Ro0, Rlsc], [Rod]))
        dq(lambda: tk.op(tk.pool, lambda e: e.tensor_tensor(out=r0[:], in0=od[:], in1=od[:], op=ALU.mult), [Rod], [Rr0]))

    def combineB(t):
        h, t0 = t["h"], t["qt"] * 512
        (r0, Rr0), (o0, Ro0), (r1, Rr1), (o1, Ro1), (od, Rod) = cb
        dq(lambda: tk.op(tk.pe, lambda e: e.matmul(SSQ[:], lhsT=onesf[:], rhs=r0[:], start=True, stop=True), [Rof, Rr0], [Rssq]))
        dq(lambda: tk.op(tk.act, lambda e: e.activation(out=r1[:], in_=SSQ[:], func=AF.Ln, scale=1.0 / 128, bias=1e-5), [Rssq], [Rr1]))
        dq(lambda: tk.op(tk.act, lambda e: e.activation(out=o1[:], in_=r1[:], func=AF.Exp, scale=-0.5), [Rr1], [Ro1]))
        dq(lambda: tk.op(tk.dve, lambda e: e.scalar_tensor_tensor(out=osb[:], in0=od[:], scalar=gsc[:, 0:1], in1=o1[:], op0=ALU.mult, op1=ALU.mult),
              [Rod, Rgsc, Ro1], [Ros]))
        dq(lambda: tk.dma(tk.pool, oT[512 + 128 * h:512 + 128 * h + 128, t0:t0 + 512], osb[:], [Ros], [RoT], Ros))

    s0(0)
    for step in range(n + 3):
        if 0 <= step - 2 < n:
            s2(step - 2)
            if (step % 128) == 0 and not tiles[step - 2]["first"]:
                warm(cx, WB, RWB, kT[:, 0:128], RkT, qT[:, 0, 0:512], RqT, 6)
        if step + 1 < n:
            s0(step + 1)
        if step < n:
            s1(step)
        for _ in range(2 if len(dqueue) > 12 else 1):
            if dqueue:
                dpop()
    while dqueue:
        dpop()
    tk.barrier()
    A.close()


def phase_mixC(cx, fm, tm, oT, Rfm, Rtm, RoT):
    nc, tk, I = cx.nc, cx.tk, cx.I
    A = Alloc(cx, "mC")
    qTs = [A.sb([128, S], BF16) for _ in range(2)]
    kTs = [A.sb([128, S], BF16) for _ in range(2)]
    vps = [A.sb([128, 64, 128], BF16) for _ in range(3)]
    cst, Rcst = A.sb([128, 4, 128], BF16)
    bias, Rbias = A.sb([128, 2, 3, 2, 128], F32)
    Uacc, RUa = A.sb([128, S], F32)
    Lacc, RLa = A.sb([128, S], F32)
    tmps = [A.sb([128, 4, 128], F32) for _ in range(4)]
    Ps = [A.sb([128, 512], BF16) for _ in range(4)]
    rl, Rrl = A.sb([128, 512], F32)
    osbs = [A.sb([128, 512], BF16) for _ in range(2)]
    Zs = [A.ps([128, 512], F32) for _ in range(4)]
    Us = [A.ps([128, 512], F32) for _ in range(2)]
    Ls = [A.ps([128, 512], F32) for _ in range(2)]
    tk.dma(tk.sp, cst[:], I["cst"][:, :, :], [], [Rcst], Rcst)
    onesb = cst[:, 3, :]
    RS = (1, 4, 16)
    mi = 0
    def loadqk(j):
        q_, Rq_ = qTs[j % 2]
        k_, Rk_ = kTs[j % 2]
        tk.dma(tk.sp, q_[:], fm[128 * j:128 * j + 128, :], [Rfm], [Rq_], Rq_)
        tk.dma(tk.sp, k_[:], fm[1024 + 128 * j:1024 + 128 * j + 128, :], [Rfm], [Rk_], Rk_)

    loadqk(0)
    for j in range(8):
        qT, RqT = qTs[j % 2]
        kT, RkT = kTs[j % 2]
        for hh in range(2):
            tk.dma(tk.sp, bias[:, hh, :, :, :], I["biasC"][2 * j + hh], [], [Rbias], Rbias)
        for ri, r in enumerate(RS):
            vp, Rvp = vps[ri]
            nb = 64 // r
            src = tm.rearrange("(b p c) d -> p c b d", p=128, c=r)
            dst = vp[:].rearrange("p (c b) d -> p c b d", c=r)
            for c in range(r):
                for b0 in range(0, nb, 16):
                    b1 = min(nb, b0 + 16)
                    tk.dma(tk.sp, dst[:, c, b0:b1, :], src[:, c, b0:b1, 128 * j:128 * j + 128], [Rtm], [Rvp], Rvp)
        if j + 1 < 8:
            loadqk(j + 1)
        mts = []
        for hh in range(2):
            for ri, r in enumerate(RS):
                nb = 64 // r
                for mt in range(16):
                    c = (4 * mt) // nb
                    b0 = (4 * mt) % nb
                    mts.append(dict(hh=hh, ri=ri, r=r, nb=nb, c=c, b0=b0, k=mi,
                                    has_prev=[(b0 + u) >= 1 for u in range(4)]))
                    mi += 1

        def cols(m, bq):
            st_ = m["c"] + m["r"] * 128 * bq
            return slice(st_, st_ + m["r"] * 127 + 1, m["r"])

        def c0(m):
            pr = slice(64 * m["hh"], 64 * m["hh"] + 64)
            Zd, RZd = Zs[(2 * m["k"]) % 4]
            Zp, RZp = Zs[(2 * m["k"] + 1) % 4]
            for u in range(4):
                bq = m["b0"] + u
                tk.op(tk.pe, lambda e, u=u, bq=bq: e.matmul(Zd[:, 128 * u:128 * u + 128], lhsT=kT[pr, cols(m, bq)], rhs=qT[pr, cols(m, bq)],
                                                            start=True, stop=True), [RkT, RqT], [RZd])
            for u in range(4):
                bq = m["b0"] + u
                if m["has_prev"][u]:
                    tk.op(tk.pe, lambda e, u=u, bq=bq: e.matmul(Zp[:, 128 * u:128 * u + 128], lhsT=kT[pr, cols(m, bq - 1)], rhs=qT[pr, cols(m, bq)],
                                                                start=True, stop=True), [RkT, RqT], [RZp])

        def c1(m):
            hh, ri = m["hh"], m["ri"]
            Zd, RZd = Zs[(2 * m["k"]) % 4]
            Zp, RZp = Zs[(2 * m["k"] + 1) % 4]
            td, Rtd = tmps[(2 * m["k"]) % 4]
            tp, Rtp = tmps[(2 * m["k"] + 1) % 4]
            Pd, RPd = Ps[(2 * m["k"]) % 4]
            Pp, RPp = Ps[(2 * m["k"] + 1) % 4]
            u0 = 0 if m["has_prev"][0] else 1
            tk.op(tk.dve, lambda e: e.tensor_tensor(out=td[:], in0=Zd[:].rearrange("p (u t) -> p u t", u=4),
                                                    in1=bias[:, hh, ri, 0:1, :].broadcast_to([128, 4, 128]), op=ALU.add),
                  [RZd, Rbias], [Rtd])
            tk.op(tk.act, lambda e: e.activation(out=Pd[:], in_=td[:].rearrange("p u t -> p (u t)"), func=AF.Exp), [Rtd], [RPd])
            tk.op(tk.dve, lambda e: e.tensor_tensor(out=tp[:, u0:4, :], in0=Zp[:, 128 * u0:512].rearrange("p (u t) -> p u t", u=4 - u0),
                                                    in1=bias[:, hh, ri, 1:2, :].broadcast_to([128, 4 - u0, 128]), op=ALU.add),
                  [RZp, Rbias], [Rtp])
            tk.op(tk.act, lambda e: e.activation(out=Pp[:, 128 * u0:512], in_=tp[:, u0:4, :].rearrange("p u t -> p (u t)"), func=AF.Exp),
                  [Rtp], [RPp])

        def c2(m):
            vp, Rvp = vps[m["ri"]]
            Pd, RPd = Ps[(2 * m["k"]) % 4]
            Pp, RPp = Ps[(2 * m["k"] + 1) % 4]
            Ub, RUb = Us[m["k"] % 2]
            Lb, RLb = Ls[m["k"] % 2]
            for u in range(4):
                B = m["c"] * m["nb"] + m["b0"] + u
                hp = m["has_prev"][u]
                us = slice(128 * u, 128 * u + 128)
                tk.op(tk.pe, lambda e, B=B, us=us, hp=hp: e.matmul(Ub[:, us], lhsT=vp[:, B, :], rhs=Pd[:, us], start=True, stop=not hp),
                      [Rvp, RPd], [RUb])
                if hp:
                    tk.op(tk.pe, lambda e, B=B, us=us: e.matmul(Ub[:, us], lhsT=vp[:, B - 1, :], rhs=Pp[:, us], start=False, stop=True),
                          [Rvp, RPp], [RUb])
                tk.op(tk.pe, lambda e, us=us, hp=hp: e.matmul(Lb[:, us], lhsT=onesb, rhs=Pd[:, us], start=True, stop=not hp),
                      [Rcst, RPd], [RLb])
                if hp:
                    tk.op(tk.pe, lambda e, us=us: e.matmul(Lb[:, us], lhsT=onesb, rhs=Pp[:, us], start=False, stop=True),
                          [Rcst, RPp], [RLb])

        def c3(m):
            pr = slice(64 * m["hh"], 64 * m["hh"] + 64)
            r = m["r"]
            Ub, RUb = Us[m["k"] % 2]
            Lb, RLb = Ls[m["k"] % 2]
            st_ = m["c"] + r * 128 * m["b0"]
            dcol = slice(st_, st_ + r * 511 + 1, r)
            if m["ri"] == 0:
                tk.op(tk.act, lambda e: e.copy(out=Uacc[pr, dcol], in_=Ub[pr, :]), [RUb], [RUa])
                tk.op(tk.act, lambda e: e.copy(out=Lacc[pr, dcol], in_=Lb[pr, :]), [RLb], [RLa])
            else:
                tk.op(tk.dve, lambda e: e.tensor_tensor(out=Uacc[pr, dcol], in0=Ub[pr, :], in1=Uacc[pr, dcol], op=ALU.add), [RUb, RUa], [RUa])
                tk.op(tk.dve, lambda e: e.tensor_tensor(out=Lacc[pr, dcol], in0=Lb[pr, :], in1=Lacc[pr, dcol], op=ALU.add), [RLb, RLa], [RLa])

        nm = len(mts)
        warm(cx, Us[0][0], Us[0][1], kT[0:64, 0:128], RkT, qT[0:64, 0:512], RqT, 24)
        c0(mts[0])
        for step in range(nm + 2):
            if step + 1 < nm:
                c0(mts[step + 1])
            if step < nm:
                c1(mts[step])
            if 0 <= step - 1 < nm:
                c2(mts[step - 1])
            if 0 <= step - 2 < nm:
                c3(mts[step - 2])
        for ch in range(NTT):
            cs = slice(512 * ch, 512 * ch + 512)
            osb, Ros = osbs[ch % 2]
            tk.op(tk.act, lambda e: e.activation(out=rl[:], in_=Lacc[:, cs], func=AF.Ln), [RLa], [Rrl])
            tk.op(tk.act, lambda e: e.activation(out=rl[:], in_=rl[:], func=AF.Exp, scale=-1.0), [Rrl], [Rrl])
            tk.op(tk.dve, lambda e: e.tensor_tensor(out=osb[:], in0=Uacc[:, cs], in1=rl[:], op=ALU.mult), [RUa, Rrl], [Ros])
            tk.dma(tk.pool, oT[128 * j:128 * j + 128, cs], osb[:], [Ros], [RoT], Ros)
    tk.barrier()
    A.close()


def phase_tail(cx, L, hsrc, Rhsrc, oT, RoT, hdst, Rhdst, Rwb, final):
    nc, tk, I, Wb = cx.nc, cx.tk, cx.I, cx.Wb
    A = Alloc(cx, "tl%d" % L)
    NSLOT = 6
    slots = [A.sb([128, 4096], BF16) for _ in range(NSLOT)]
    gts = [A.sb([128, D], F32) for _ in range(3 if final else 2)]
    ident, Rid = A.sb([128, 128], BF16)
    hts = [A.sb([128, 4, D], F32) for _ in range(2)]
    ots = [A.sb([128, 8, 512], BF16) for _ in range(2)]
    pts = [A.sb([128, 4, 256], F32) for _ in range(2)]
    junk, Rjunk = A.sb([128, D], BF16)
    sts = [A.sb([128, 8], F32) for _ in range(3)]
    hns = [A.sb([128, 4, D], BF16) for _ in range(2)]
    hnTs = [A.sb([128, 8, 512], BF16) for _ in range(2)]
    uT, RuT = A.sb([128, 32, 512], BF16)
    rls = [A.sb([128, 512], F32) for _ in range(2)]
    gsb = [A.sb([128, 512], F32) for _ in range(2)]
    pbf, Rpbf = A.sb([128, 4, 256], BF16)
    pT, RpT = A.sb([128, 2, 512], BF16)
    psTs = Rot([A.ps([128, 1024], BF16) for _ in range(2)])
    banks = Rot([A.ps([128, 512], F32) for _ in range(6)])

    tk.dma(tk.sp, ident[:], I["cst"][:, 0, :], [], [Rid], Rid)
    tk.dma(tk.sp, gts[0][0][:], I["norm_mlp_g"][L:L + 1, :].broadcast_to([128, D]), [], [gts[0][1]], gts[0][1])
    tk.dma(tk.sp, gts[1][0][:], I["norm_ple_g"][L:L + 1, :].broadcast_to([128, D]), [], [gts[1][1]], gts[1][1])
    if final:
        tk.dma(tk.sp, gts[2][0][:], I["final_norm_g"][0:1, :].broadcast_to([128, D]), [], [gts[2][1]], gts[2][1])

    wo = Wb["out", L].rearrange("(c p) n -> p c n", p=128)
    wu = Wb["up", L].rearrange("(c p) n -> p c n", p=128)
    wd = Wb["down", L].rearrange("(f p) n -> p f n", p=128)
    wg = Wb["gate", L].rearrange("(c p) n -> p c n", p=128)
    wp_ = Wb["ple", L].rearrange("(c p) n -> p c n", p=128)

    def p_out():
        return [("out", wo[:, 4 * q:4 * q + 4, :], (4, 1024)) for q in range(2)]

    plist = p_out()
    for tt in range(NTT):
        plist += [("up", wu[:, :, 512 * q:512 * q + 512], (8, 512)) for q in range(8)]
        for half in range(2):
            plist += [("down", wd[:, 8 * q:8 * q + 8, 512 * half:512 * half + 512], (8, 512)) for q in range(4)]
        if tt + 1 < NTT:
            plist += p_out()
        plist += [("gate", wg[:, 4 * q:4 * q + 4, :], (4, 1024)) for q in range(2)]
        plist += [("ple", wp_[:, :, :], (2, 1024))]
    total = len(plist)
    state = dict(issued=0, taken=0)

    def issue_upto(k):
        while state["issued"] < min(k, total):
            g = state["issued"]
            nm, ap, (a, b) = plist[g]
            sl, Rsl = slots[g % NSLOT]
            tk.dma(tk.sp, sl[:, 0:a * b].rearrange("p (a b) -> p a b", a=a), ap, [Rwb], [Rsl], Rsl)
            state["issued"] += 1

    def take(kind):
        g = state["taken"]
        state["taken"] += 1
        assert g < state["issued"], (g, state)
        nm, ap, (a, b) = plist[g]
        assert nm == kind, (nm, kind)
        sl, Rsl = slots[g % NSLOT]
        return g, sl[:, 0:a * b].rearrange("p (a b) -> p a b", a=a), Rsl

    def done(g):
        issue_upto(g + NSLOT + 1)

    def load(tt):
        t0 = tt * 512
        ht, Rht = hts[tt % 2]
        ot, Rot_ = ots[tt % 2]
        pt, Rpt = pts[tt % 2]
        tk.dma(tk.sp, ht[:], hsrc[t0:t0 + 512, :].rearrange("(j p) d -> p j d", p=128), [Rhsrc] if Rhsrc is not None else [], [Rht], Rht)
        tk.dma(tk.sp, ot[:], oT[:, t0:t0 + 512].rearrange("(c p) t -> p c t", p=128), [RoT], [Rot_], Rot_)
        tk.dma(tk.sp, pt[:], I["p"][L, t0:t0 + 512, :].rearrange("(j p) d -> p j d", p=128), [], [Rpt], Rpt)

    def stageA(tt):
        ht, Rht = hts[tt % 2]
        ot, Rot_ = ots[tt % 2]
        g0, w0, Rw0 = take("out")
        g1, w1, Rw1 = take("out")
        for j in range(4):
            for half in range(2):
                bk, Rb = banks.next()
                for c in range(8):
                    wv, Rwv = (w0, Rw0) if c < 4 else (w1, Rw1)
                    tk.op(tk.pe, lambda e, c=c, wv=wv, bk=bk, j=j, half=half: e.matmul(
                        bk[:], lhsT=ot[:, c, j * 128:(j + 1) * 128], rhs=wv[:, c % 4, 512 * half:512 * half + 512],
                        start=(c == 0), stop=(c == 7)), [Rot_, Rwv], [Rb])
                hs = ht[:, j, 512 * half:512 * half + 512]
                tk.op(tk.dve, lambda e, bk=bk, hs=hs: e.tensor_tensor(out=hs, in0=bk[:], in1=hs, op=ALU.add), [Rb, Rht], [Rht])
        done(g1)

    def stageN(tt, which):
        ht, Rht = hts[tt % 2]
        st, Rst = sts[which]
        hn, Rhn = hns[which]
        gt, Rg = gts[which]
        rms_stats(cx, ht, Rht, st, Rst, junk, Rjunk, 4, 1e-6, D)
        for j in range(4):
            tk.op(tk.dve, lambda e, j=j: e.scalar_tensor_tensor(out=hn[:, j, :], in0=ht[:, j, :], scalar=st[:, 4 + j:5 + j],
                                                                in1=gt[:], op0=ALU.mult, op1=ALU.mult), [Rht, Rst, Rg], [Rhn])

    def stageT(tt, which):
        hn, Rhn = hns[which]
        hnT, RhnT = hnTs[which]
        for j in range(4):
            psT, RpsT = psTs.next()
            for c in range(8):
                tk.op(tk.pe, lambda e, j=j, c=c, psT=psT: e.transpose(out=psT[:, c * 128:(c + 1) * 128],
                                                                     in_=hn[:, j, c * 128:(c + 1) * 128], identity=ident[:]),
                      [Rhn, Rid], [RpsT])
            src = psT[:].rearrange("p (c t) -> p c t", c=8)
            dst = hnT[:, :, j * 128:(j + 1) * 128]
            if j % 2 == 0:
                tk.op(tk.act, lambda e, src=src, dst=dst: e.copy(out=dst, in_=src), [RpsT], [RhnT])
            else:
                tk.op(tk.dve, lambda e, src=src, dst=dst: e.tensor_copy(out=dst, in_=src), [RpsT], [RhnT])

    def stageC(tt):
        hnT, RhnT = hnTs[0]
        for q in range(8):
            g, wv, Rwv = take("up")
            for fl in range(4):
                f = 4 * q + fl
                bk, Rb = banks.next()
                for c in range(8):
                    tk.op(tk.pe, lambda e, c=c, wv=wv, bk=bk, fl=fl: e.matmul(bk[:], lhsT=wv[:, c, 128 * fl:128 * fl + 128], rhs=hnT[:, c, :],
                                                                              start=(c == 0), stop=(c == 7)), [Rwv, RhnT], [Rb])
                rl, Rrl = rls[f % 2]
                tk.op(tk.act, lambda e, bk=bk, rl=rl: e.activation(out=rl[:], in_=bk[:], func=AF.Relu), [Rb], [Rrl])
                tk.op(tk.dve, lambda e, rl=rl, f=f: e.tensor_tensor(out=uT[:, f, :], in0=rl[:], in1=rl[:], op=ALU.mult), [Rrl], [RuT])
            done(g)

    def stageD(tt):
        ht, Rht = hts[tt % 2]
        for half in range(2):
            bks = [banks.next() for _ in range(4)]
            for q in range(4):
                g, wv, Rwv = take("down")
                for j in range(4):
                    bk, Rb = bks[j]
                    for fl in range(8):
                        f = 8 * q + fl
                        tk.op(tk.pe, lambda e, bk=bk, wv=wv, fl=fl, f=f, j=j: e.matmul(bk[:], lhsT=uT[:, f, j * 128:(j + 1) * 128], rhs=wv[:, fl, :],
                                                                                      start=(f == 0), stop=(f == 31)), [RuT, Rwv], [Rb])
                done(g)
            for j in range(4):
                bk, Rb = bks[j]
                hs = ht[:, j, 512 * half:512 * half + 512]
                tk.op(tk.dve, lambda e, bk=bk, hs=hs: e.tensor_tensor(out=hs, in0=bk[:], in1=hs, op=ALU.add), [Rb, Rht], [Rht])

    evc = [0]

    def stageP(tt):
        pt, Rpt = pts[tt % 2]
        tk.op(tk.pool, lambda e: e.tensor_copy(out=pbf[:], in_=pt[:]), [Rpt], [Rpbf])
        for j in range(4):
            psT, RpsT = psTs.next()
            for c in range(2):
                tk.op(tk.pe, lambda e, j=j, c=c, psT=psT: e.transpose(out=psT[:, c * 128:(c + 1) * 128], in_=pbf[:, j, c * 128:(c + 1) * 128],
                                                                     identity=ident[:]), [Rpbf, Rid], [RpsT])
            tk.op(tk.act, lambda e, j=j, psT=psT: e.copy(out=pT[:, :, j * 128:(j + 1) * 128],
                                                         in_=psT[:, 0:256].rearrange("p (c t) -> p c t", c=2)), [RpsT], [RpT])

    def stageE(tt):
        ht, Rht = hts[tt % 2]
        hnT, RhnT = hnTs[1]
        g18, wg0, Rwg0 = take("gate")
        g19, wg1, Rwg1 = take("gate")
        g20, wp, Rwp = take("ple")
        for j in range(4):
            for half in range(2):
                bk, Rb = banks.next()
                for c in range(8):
                    wv, Rwv = (wg0, Rwg0) if c < 4 else (wg1, Rwg1)
                    tk.op(tk.pe, lambda e, c=c, wv=wv, bk=bk, j=j, half=half: e.matmul(
                        bk[:], lhsT=hnT[:, c, j * 128:(j + 1) * 128], rhs=wv[:, c % 4, 512 * half:512 * half + 512],
                        start=(c == 0), stop=(c == 7)), [RhnT, Rwv], [Rb])
                evc[0] += 1
                gs, Rgs = gsb[evc[0] % 2]
                tk.op(tk.act, lambda e, bk=bk, gs=gs: e.activation(out=gs[:], in_=bk[:], func=AF.Sigmoid), [Rb], [Rgs])
                bk2, Rb2 = banks.next()
                for c in range(2):
                    tk.op(tk.pe, lambda e, c=c, bk2=bk2, j=j, half=half: e.matmul(
                        bk2[:], lhsT=pT[:, c, j * 128:(j + 1) * 128], rhs=wp[:, c, 512 * half:512 * half + 512],
                        start=(c == 0), stop=(c == 1)), [RpT, Rwp], [Rb2])
                tk.op(tk.dve, lambda e, bk2=bk2, gs=gs: e.tensor_tensor(out=gs[:], in0=bk2[:], in1=gs[:], op=ALU.mult), [Rb2, Rgs], [Rgs])
                hs = ht[:, j, 512 * half:512 * half + 512]
                tk.op(tk.pool, lambda e, gs=gs, hs=hs: e.tensor_tensor(out=hs, in0=gs[:], in1=hs, op=ALU.add), [Rgs, Rht], [Rht])
        done(g20)

    def stageF(tt):
        t0 = tt * 512
        ht, Rht = hts[tt % 2]
        if final:
            st, Rst = sts[2]
            rms_stats(cx, ht, Rht, st, Rst, junk, Rjunk, 4, 1e-6, D)
            for j in range(4):
                tk.op(tk.dve, lambda e, j=j: e.scalar_tensor_tensor(out=ht[:, j, :], in0=ht[:, j, :], scalar=st[:, 4 + j:5 + j],
                                                                    in1=gts[2][0][:], op0=ALU.mult, op1=ALU.mult),
                      [Rht, Rst, gts[2][1]], [Rht])
        tk.dma(tk.pool, hdst[t0:t0 + 512, :].rearrange("(j p) d -> p j d", p=128), ht[:], [Rht], [Rhdst], Rht)

    issue_upto(2)
    load(0)
    issue_upto(NSLOT)
    stageA(0)
    stageN(0, 0)
    stageT(0, 0)
    for tt in range(NTT):
        nxt = tt + 1 < NTT
        if nxt:
            load(tt + 1)
        stageC(tt)
        stageP(tt)
        stageD(tt)
        stageN(tt, 1)
        if nxt:
            stageA(tt + 1)
        stageT(tt, 1)
        if nxt:
            stageN(tt + 1, 0)
        stageE(tt)
        if nxt:
            stageT(tt + 1, 0)
        stageF(tt)
    assert state["taken"] == total
    tk.barrier()
    A.close()


_CACHE = {}


def make_in_maps(inputs):
    f = lambda a: np.ascontiguousarray(np.asarray(a, dtype=np.float32))
    hc = host_consts(inputs["t5_table"])
    shared = {
        "t5_table": f(inputs["t5_table"]),
        "w_in_even": f(inputs["w_in_even"][0]),
        "w_in_odd": f(inputs["w_in_odd"][0]),
        "w_out": f(np.stack([np.asarray(inputs["w_out_even"][0]), np.asarray(inputs["w_out_odd"][0])])),
        "lam": f(np.concatenate([np.asarray(inputs[k]) for k in ("lambda_q1", "lambda_k1", "lambda_q2", "lambda_k2")], axis=0)),
        "subln_g": f(np.asarray(inputs["subln_g"]).reshape(128, 1)),
        "norm_mix_g": f(inputs["norm_mix_g"]),
        "norm_mlp_g": f(inputs["norm_mlp_g"]),
        "norm_ple_g": f(inputs["norm_ple_g"]),
        "final_norm_g": f(np.asarray(inputs["final_norm_g"]).reshape(1, D)),
        "w_mlp_up": f(inputs["w_mlp_up"]),
        "w_mlp_down": f(inputs["w_mlp_down"]),
        "w_ple_gate": f(inputs["w_ple_gate"]),
        "w_ple_proj": f(inputs["w_ple_proj"]),
        "cst": hc["cst"], "onesf": hc["onesf"], "maskA": hc["maskA"], "biasB": hc["biasB"], "biasC": hc["biasC"],
    }
    x = np.asarray(inputs["x"], dtype=np.float32)
    p = np.asarray(inputs["p"], dtype=np.float32)
    maps = []
    for b in range(8):
        m = dict(shared)
        m["x"] = np.ascontiguousarray(x[b])
        m["p"] = np.ascontiguousarray(p[:, b])
        maps.append(m)
    return maps


def kernel(**inputs):
    if "nc" not in _CACHE:
        _CACHE["nc"] = build_program()
    nc = _CACHE["nc"]
    maps = make_in_maps(inputs)
    res = run_bass_kernel_spmd(nc, maps, core_ids=list(range(8)))
    return np.stack([np.asarray(r["out"], dtype=np.float32) for r in res.results], axis=0)
```

```python
import math
import numpy as np
import ml_dtypes
import concourse.bass as bass
import concourse.mybir as mybir
from concourse.bass_utils import run_bass_kernel_spmd

F32 = mybir.dt.float32
BF16 = mybir.dt.bfloat16
AF = mybir.ActivationFunctionType
ALU = mybir.AluOpType

S = 8192
D = 1024
NTT = S // 512
NEG = -1e30
LAMBDA_INIT0 = 0.8 - 0.6 * math.exp(-0.3 * 0)


class Res:
    __slots__ = ("name", "w", "r", "dsem", "dcnt")

    def __init__(self, name):
        self.name = name
        self.w = None
        self.r = {}
        self.dsem = None
        self.dcnt = 0


class Eng:
    def __init__(self, nc, eng, name, is_pe=False):
        self.eng = eng
        self.name = name
        self.sem = nc.alloc_semaphore("sem_" + name)
        self.cnt = 0
        self.seen = {}
        self.is_pe = is_pe


class Tracker:
    def __init__(self, nc):
        self.nc = nc
        self.pe = Eng(nc, nc.tensor, "pe", is_pe=True)
        self.act = Eng(nc, nc.scalar, "act")
        self.dve = Eng(nc, nc.vector, "dve")
        self.pool = Eng(nc, nc.gpsimd, "pool")
        self.sp = Eng(nc, nc.sync, "sp")
        self.engs = [self.pe, self.act, self.dve, self.pool, self.sp]
        self.nd = 0
        self.free_dsems = []
        self.live = []

    def res(self, name):
        return Res(name)

    def give_dsem(self, r):
        if r.dsem is None:
            if self.free_dsems:
                r.dsem, r.dcnt = self.free_dsems.pop()
            else:
                self.nd += 1
                r.dsem = self.nc.alloc_semaphore("ds%d" % self.nd)
                r.dcnt = 0
            self.live.append(r)

    def _collect(self, E, reads, writes):
        need = {}

        def add(t, raw):
            if t is None:
                return
            sem, val, is_dma, owner = t
            if sem is E.sem:
                if E.is_pe or not raw:
                    return
            if is_dma:
                val = 16 * owner.dcnt
            k = id(sem)
            if k not in need or need[k][1] < val:
                need[k] = (sem, val)

        for r in reads:
            if r is not None:
                add(r.w, True)
        for w in writes:
            if w is None:
                continue
            add(w.w, False)
            for t in w.r.values():
                add(t, False)
        for k, (sem, val) in need.items():
            if E.seen.get(k, 0) < val:
                E.eng.wait_ge(sem, val)
                E.seen[k] = val

    def _record(self, tk, reads, writes):
        k = id(tk[0])
        reads = [r for r in reads if r is not None]
        writes = [w for w in writes if w is not None]
        for r in reads:
            old = r.r.get(k)
            if old is None or old[1] < tk[1]:
                r.r[k] = tk
        for w in writes:
            w.w = tk
            w.r = {}

    def op(self, E, fn, reads=(), writes=()):
        self._collect(E, reads, writes)
        ins = fn(E.eng)
        E.cnt += 1
        ins.then_inc(E.sem, 1)
        self._record((E.sem, E.cnt, False, None), reads, writes)

    def dma(self, Q, out, in_, reads, writes, dres):
        self.give_dsem(dres)
        self._collect(Q, reads, writes)
        ins = Q.eng.dma_start(out=out, in_=in_)
        dres.dcnt += 1
        ins.then_inc(dres.dsem, 16)
        self._record((dres.dsem, 16 * dres.dcnt, True, dres), reads, writes)

    def barrier(self, release=True):
        for E in self.engs:
            for O in self.engs:
                if O is E or O.cnt == 0:
                    continue
                k = id(O.sem)
                if E.seen.get(k, 0) < O.cnt:
                    E.eng.wait_ge(O.sem, O.cnt)
                    E.seen[k] = O.cnt
            for r in self.live:
                k = id(r.dsem)
                v = 16 * r.dcnt
                if v > 0 and E.seen.get(k, 0) < v:
                    E.eng.wait_ge(r.dsem, v)
                    E.seen[k] = v
        if release:
            for r in self.live:
                self.free_dsems.append((r.dsem, r.dcnt))
                r.dsem = None
            self.live = []


class Rot:
    def __init__(self, items):
        self.items = items
        self.i = 0

    def next(self):
        it = self.items[self.i % len(self.items)]
        self.i += 1
        return it


class Cx:
    pass


def t5_bucket_np(dist):
    n = np.maximum(dist, 0)
    nf = np.maximum(n, 1).astype(np.float32)
    large = 16 + (np.log(nf / np.float32(16)) / np.float32(math.log(128 / 16)) * np.float32(16)).astype(np.int32)
    large = np.minimum(large, 31)
    return np.where(n < 16, n, large)


def host_consts(t5_table):
    bf = ml_dtypes.bfloat16
    c = {}
    j = np.arange(128)
    cst = np.zeros((128, 4, 128), np.float32)
    cst[:, 0, :] = np.eye(128)
    cst[:, 1, :] = -(j[:, None] >= j[None, :]).astype(np.float32)
    cst[:, 2, :] = -1.0
    cst[:, 3, :] = 1.0
    c["cst"] = cst.astype(bf)
    c["onesf"] = np.ones((128, 128), np.float32)
    r = np.arange(128)[:, None]
    q = np.arange(512)[None, :]
    mA = np.zeros((128, 4, 512), np.float32)
    for i in range(4):
        mA[:, i, :] = np.where(128 * i + r < q, 0.0, NEG)
    c["maskA"] = mA.astype(bf)
    tab = np.asarray(t5_table, np.float32)
    bB = np.zeros((8, 128, 5, 512), np.float32)
    for ib in range(5):
        i = ib - 1
        dist = q - (128 * i + r)
        bk = t5_bucket_np(dist)
        for sl in range(8):
            g = tab[bk, 8 + sl]
            bB[sl, :, ib, :] = np.where(dist >= 0, g, np.float32(NEG))
    c["biasB"] = bB
    bC = np.zeros((16, 128, 3, 2, 128), np.float32)
    s_ = np.arange(128)[:, None]
    t_ = np.arange(128)[None, :]
    for ri, rr in enumerate((1, 4, 16)):
        dd = rr * (t_ - s_)
        dp = rr * (128 + t_ - s_)
        for hh in range(16):
            bC[hh, :, ri, 0, :] = np.where(dd >= 0, tab[t5_bucket_np(dd), hh], np.float32(NEG))
            bC[hh, :, ri, 1, :] = np.where(dp <= 128 * rr, tab[t5_bucket_np(dp), hh], np.float32(NEG))
    c["biasC"] = bC
    return c


def build_program(nphase=7, dbg=False):
    nc = bass.Bass("TRN2", target_bir_lowering=False)
    tk = Tracker(nc)
    cx = Cx()
    cx.nc, cx.tk = nc, tk

    def din(name, shape, dt=F32):
        return nc.dram_tensor(name, list(shape), dt, kind="ExternalInput").ap()

    def dscr(name, shape, dt):
        return nc.dram_tensor(name, list(shape), dt, kind=("ExternalOutput" if dbg else "Internal")).ap()

    I = {}
    I["x"] = din("x", [S, D])
    I["p"] = din("p", [2, S, 256])
    I["t5_table"] = din("t5_table", [32, 16])
    I["w_in_even"] = din("w_in_even", [D, 3072])
    I["w_in_odd"] = din("w_in_odd", [D, 3072])
    I["w_out"] = din("w_out", [2, D, D])
    I["lam"] = din("lam", [4, 64])
    I["subln_g"] = din("subln_g", [128, 1])
    I["norm_mix_g"] = din("norm_mix_g", [2, D])
    I["norm_mlp_g"] = din("norm_mlp_g", [2, D])
    I["norm_ple_g"] = din("norm_ple_g", [2, D])
    I["final_norm_g"] = din("final_norm_g", [1, D])
    I["w_mlp_up"] = din("w_mlp_up", [2, D, 4096])
    I["w_mlp_down"] = din("w_mlp_down", [2, 4096, D])
    I["w_ple_gate"] = din("w_ple_gate", [2, D, D])
    I["w_ple_proj"] = din("w_ple_proj", [2, 256, D])
    I["cst"] = din("cst", [128, 4, 128], BF16)
    I["onesf"] = din("onesf", [128, 128])
    I["maskA"] = din("maskA", [128, 4, 512], BF16)
    I["biasB"] = din("biasB", [8, 128, 5, 512])
    I["biasC"] = din("biasC", [16, 128, 3, 2, 128])
    out = nc.dram_tensor("out", [S, D], F32, kind="ExternalOutput").ap()
    cx.I = I

    Wb = {}
    for l in range(2):
        Wb["out", l] = dscr("wb_out%d" % l, [D, D], BF16)
        Wb["up", l] = dscr("wb_up%d" % l, [D, 4096], BF16)
        Wb["down", l] = dscr("wb_down%d" % l, [4096, D], BF16)
        Wb["gate", l] = dscr("wb_gate%d" % l, [D, D], BF16)
        Wb["ple", l] = dscr("wb_ple%d" % l, [256, D], BF16)
    cx.Wb = Wb
    fm = [dscr("fm%d" % l, [2048, S], BF16) for l in range(2)]
    tm = [dscr("tm%d" % l, [S, D], BF16) for l in range(2)]
    oT = [dscr("oT%d" % l, [D, S], BF16) for l in range(2)]
    h1 = dscr("h1", [S, D], F32)
    R_fm = [None, None]
    R_tm = [None, None]
    R_oT = [None, None]
    R_h1 = None
    R_out = None
    R_wb = None
    R_wbd = tk.res("wb")

    srcs = {"out": I["w_out"], "up": I["w_mlp_up"], "down": I["w_mlp_down"], "gate": I["w_ple_gate"], "ple": I["w_ple_proj"]}
    win1_b = dscr("wb_in1", [D, 3072], BF16)

    def prologue():
        for r0 in range(0, D, 128):
            tk.dma(tk.pool, win1_b[r0:r0 + 128, :], I["w_in_odd"][r0:r0 + 128, :], [], [], R_wbd)
        for l in range(2):
            for nm in ("out", "up", "down", "gate", "ple"):
                src = srcs[nm][l]
                dst = Wb[nm, l]
                rows = src.shape[0]
                for r0 in range(0, rows, 128):
                    tk.dma(tk.pool, dst[r0:r0 + 128, :], src[r0:r0 + 128, :], [], [], R_wbd)

    phase_proj(cx, 0, I["x"], None, I["norm_mix_g"][0:1, :], I["w_in_even"], True, prologue, fm[0], tm[0], R_fm[0], R_tm[0],
               [(128 * i, 128 * i, 0.125) for i in range(4)] + [(512 + 128 * i, 512 + 128 * i, 1.0) for i in range(4)]
               + [(1536 + 128 * i, 1024 + 128 * i, 0.125) for i in range(4)] + [(2048 + 128 * i, 1536 + 128 * i, 1.0) for i in range(4)],
               [(1024, 0), (2560, 512)])
    tk.barrier()
    if nphase >= 2:
        phase_mixA(cx, fm[0], tm[0], oT[0], R_fm[0], R_tm[0], R_oT[0])
        tk.barrier()
    if nphase >= 3:
        phase_mixB(cx, fm[0], tm[0], oT[0], R_fm[0], R_tm[0], R_oT[0])
        tk.barrier()
    if nphase >= 4:
        phase_tail(cx, 0, I["x"], None, oT[0], R_oT[0], h1, R_h1, R_wb, final=False)
        tk.barrier()
    if nphase >= 5:
        phase_proj(cx, 1, h1, R_h1, I["norm_mix_g"][1:2, :], win1_b, False, None, fm[1], tm[1], R_fm[1], R_tm[1],
                   [(128 * i, 128 * i, 0.125) for i in range(8)] + [(1024 + 128 * i, 1024 + 128 * i, 1.0) for i in range(8)],
                   [(2048, 0), (2560, 512)])
        tk.barrier()
    if nphase >= 6:
        phase_mixC(cx, fm[1], tm[1], oT[1], R_fm[1], R_tm[1], R_oT[1])
        tk.barrier()
    if nphase >= 7:
        phase_tail(cx, 1, h1, R_h1, oT[1], R_oT[1], out, R_out, R_wb, final=True)
    tk.barrier(release=False)
    return nc


class Alloc:
    def __init__(self, cx, tag):
        import contextlib
        self.cx = cx
        self.tag = tag
        self.stk = contextlib.ExitStack()
        self.n = 0

    def sb(self, shape, dt, name=None):
        self.n += 1
        h = self.stk.enter_context(self.cx.nc.sbuf_tensor("%s_s%d" % (self.tag, self.n), list(shape), dt))
        return h, self.cx.tk.res("%s_s%d" % (self.tag, self.n))

    def ps(self, shape, dt):
        self.n += 1
        h = self.stk.enter_context(self.cx.nc.psum_tensor("%s_p%d" % (self.tag, self.n), list(shape), dt))
        return h, self.cx.tk.res("%s_p%d" % (self.tag, self.n))

    def close(self):
        self.stk.close()


def rms_stats(cx, xt, Rx, st, Rst, junk, Rjunk, nj, eps, width):
    tk = cx.tk
    for j in range(nj):
        tk.op(tk.act, lambda e, j=j: e.activation(out=junk[:, 0:width], in_=xt[:, j, :], func=AF.Square,
                                                  accum_out=st[:, j:j + 1]), [Rx], [Rjunk, Rst])
    tk.op(tk.act, lambda e: e.activation(out=st[:, nj:2 * nj], in_=st[:, 0:nj], func=AF.Sqrt,
                                         scale=1.0 / width, bias=eps), [Rst], [Rst])
    tk.op(tk.dve, lambda e: e.reciprocal(out=st[:, nj:2 * nj], in_=st[:, nj:2 * nj]), [Rst], [Rst])


def norm_transpose(cx, xt, Rx, gt, Rg, st, Rst, hn, Rhn, hnT, RhnT, psTs, ident, Rid, flip):
    tk = cx.tk
    for j in range(4):
        tk.op(tk.dve, lambda e, j=j: e.scalar_tensor_tensor(out=hn[:, j, :], in0=xt[:, j, :], scalar=st[:, 4 + j:5 + j],
                                                            in1=gt[:], op0=ALU.mult, op1=ALU.mult),
              [Rx, Rst, Rg], [Rhn])
    for j in range(4):
        psT, RpsT = psTs.next()
        for c in range(8):
            tk.op(tk.pe, lambda e, j=j, c=c, psT=psT: e.transpose(out=psT[:, c * 128:(c + 1) * 128],
                                                                 in_=hn[:, j, c * 128:(c + 1) * 128], identity=ident[:]),
                  [Rhn, Rid], [RpsT])
        src = psT[:].rearrange("p (c t) -> p c t", c=8)
        dst = hnT[:, :, j * 128:(j + 1) * 128]
        if (j + flip) % 2 == 0:
            tk.op(tk.act, lambda e, src=src, dst=dst: e.copy(out=dst, in_=src), [RpsT], [RhnT])
        else:
            tk.op(tk.dve, lambda e, src=src, dst=dst: e.tensor_copy(out=dst, in_=src), [RpsT], [RhnT])


def warm(cx, bank, Rbank, lhsT, RlhsT, rhs, Rrhs, n):
    tk = cx.tk
    for i in range(n):
        tk.op(tk.pe, lambda e: e.matmul(bank[:], lhsT=lhsT, rhs=rhs, start=True, stop=True), [RlhsT, Rrhs], [Rbank])


def phase_proj(cx, L, src, Rsrc, gvec, win, win_cast, after_setup, fm, tm, Rfm, Rtm, fmchunks, tmgroups):
    nc, tk, I = cx.nc, cx.tk, cx.I
    A = Alloc(cx, "pj%d" % L)
    wsb, Rw = A.sb([128, 8, 3072], BF16)
    gt, Rg = A.sb([128, D], F32)
    ident, Rid = A.sb([128, 128], BF16)
    xts = [A.sb([128, 4, D], F32) for _ in range(2)]
    junk, Rjunk = A.sb([128, D], BF16)
    sts = [A.sb([128, 8], F32) for _ in range(2)]
    hns = [A.sb([128, 4, D], BF16) for _ in range(2)]
    hnTs = [A.sb([128, 8, 512], BF16) for _ in range(2)]
    stg = Rot([A.sb([128, 512], BF16) for _ in range(6)])
    psTs = Rot([A.ps([128, 1024], BF16) for _ in range(2)])
    banks = Rot([A.ps([128, 512], F32) for _ in range(6)])

    def load(tt):
        xt, Rx = xts[tt % 2]
        tk.dma(tk.sp, xt[:], src[tt * 512:(tt + 1) * 512, :].rearrange("(j p) d -> p j d", p=128),
               [Rsrc] if Rsrc is not None else [], [Rx], Rx)

    tk.dma(tk.sp, gt[:], gvec.broadcast_to([128, D]), [], [Rg], Rg)
    tk.dma(tk.sp, ident[:], I["cst"][:, 0, :], [], [Rid], Rid)
    load(0)
    if win_cast:
        wst = [A.sb([128, 3072], F32) for _ in range(2)]
        Rws = [tk.res("wsbc%d" % c) for c in range(8)]
        for c in range(8):
            ws_, Rs_ = wst[c % 2]
            tk.dma(tk.sp, ws_[:], win[c * 128:(c + 1) * 128, :], [], [Rs_], Rs_)
            if c % 2 == 0:
                tk.op(tk.dve, lambda e, c=c, ws_=ws_: e.tensor_copy(out=wsb[:, c, :], in_=ws_[:]), [Rs_], [Rws[c]])
            else:
                tk.op(tk.act, lambda e, c=c, ws_=ws_: e.copy(out=wsb[:, c, :], in_=ws_[:]), [Rs_], [Rws[c]])
    else:
        Rws = [Rw] * 8
        for c in range(8):
            tk.dma(tk.sp, wsb[:, c, :], win[c * 128:(c + 1) * 128, :], [], [Rw], Rw)
    if after_setup is not None:
        after_setup()

    def stageN(tt):
        xt, Rx = xts[tt % 2]
        st, Rst = sts[tt % 2]
        hn, Rhn = hns[tt % 2]
        rms_stats(cx, xt, Rx, st, Rst, junk, Rjunk, 4, 1e-6, D)
        for j in range(4):
            tk.op(tk.dve, lambda e, j=j: e.scalar_tensor_tensor(out=hn[:, j, :], in0=xt[:, j, :], scalar=st[:, 4 + j:5 + j],
                                                                in1=gt[:], op0=ALU.mult, op1=ALU.mult), [Rx, Rst, Rg], [Rhn])

    def stageT(tt):
        hn, Rhn = hns[tt % 2]
        hnT, RhnT = hnTs[tt % 2]
        for j in range(4):
            psT, RpsT = psTs.next()
            for c in range(8):
                tk.op(tk.pe, lambda e, j=j, c=c, psT=psT: e.transpose(out=psT[:, c * 128:(c + 1) * 128],
                                                                     in_=hn[:, j, c * 128:(c + 1) * 128], identity=ident[:]),
                      [Rhn, Rid], [RpsT])
            srcp = psT[:].rearrange("p (c t) -> p c t", c=8)
            dst = hnT[:, :, j * 128:(j + 1) * 128]
            if j % 2 == 0:
                tk.op(tk.act, lambda e, srcp=srcp, dst=dst: e.copy(out=dst, in_=srcp), [RpsT], [RhnT])
            else:
                tk.op(tk.dve, lambda e, srcp=srcp, dst=dst: e.tensor_copy(out=dst, in_=srcp), [RpsT], [RhnT])

    ev = [0]

    def stageM(tt):
        t0 = tt * 512
        hnT, RhnT = hnTs[tt % 2]
        for (wc, fr, sc) in fmchunks:
            bk, Rb = banks.next()
            for c in range(8):
                tk.op(tk.pe, lambda e, c=c, bk=bk, wc=wc: e.matmul(bk[:], lhsT=wsb[:, c, wc:wc + 128], rhs=hnT[:, c, :],
                                                                    start=(c == 0), stop=(c == 7)), [Rws[c], RhnT], [Rb])
            sg, Rsg = stg.next()
            ev[0] += 1
            if ev[0] % 2 == 0:
                tk.op(tk.act, lambda e, sg=sg, bk=bk, sc=sc: e.activation(out=sg[:], in_=bk[:], func=AF.Copy, scale=sc), [Rb], [Rsg])
            else:
                tk.op(tk.dve, lambda e, sg=sg, bk=bk, sc=sc: e.tensor_scalar_mul(out=sg[:], in0=bk[:], scalar1=sc), [Rb], [Rsg])
            tk.dma(tk.pool, fm[fr:fr + 128, t0:t0 + 512], sg[:], [Rsg], [Rfm], Rsg)
        for j in range(4):
            for (wc, tc) in tmgroups:
                bk, Rb = banks.next()
                for c in range(8):
                    tk.op(tk.pe, lambda e, c=c, bk=bk, wc=wc, j=j: e.matmul(bk[:], lhsT=hnT[:, c, j * 128:(j + 1) * 128],
                                                                            rhs=wsb[:, c, wc:wc + 512], start=(c == 0), stop=(c == 7)),
                          [Rws[c], RhnT], [Rb])
                sg, Rsg = stg.next()
                ev[0] += 1
                if ev[0] % 2 == 0:
                    tk.op(tk.act, lambda e, sg=sg, bk=bk: e.copy(out=sg[:], in_=bk[:]), [Rb], [Rsg])
                else:
                    tk.op(tk.dve, lambda e, sg=sg, bk=bk: e.tensor_copy(out=sg[:], in_=bk[:]), [Rb], [Rsg])
                tk.dma(tk.pool, tm[t0 + j * 128:t0 + (j + 1) * 128, tc:tc + 512], sg[:], [Rsg], [Rtm], Rsg)

    stageN(0)
    stageT(0)
    for tt in range(NTT):
        if tt + 1 < NTT:
            load(tt + 1)
            stageN(tt + 1)
        stageM(tt)
        if tt + 1 < NTT:
            stageT(tt + 1)
    tk.barrier()
    A.close()


def phase_mixA(cx, fm, tm, oT, Rfm, Rtm, RoT):
    nc, tk, I = cx.nc, cx.tk, cx.I
    A = Alloc(cx, "mA")
    va, Rva = A.sb([128, 64, 512], BF16)
    qTs = [A.sb([128, S], BF16) for _ in range(2)]
    kTs = [A.sb([128, S], BF16) for _ in range(2)]
    for (t_, R_) in qTs + kTs:
        tk.op(tk.pool, lambda e, t_=t_: e.memset(t_[64:128, :], 0.0), [], [R_])

    def loadqk(h):
        tk.dma(tk.sp, qTs[h % 2][0][0:64, :], fm[64 * h:64 * h + 64, :], [Rfm], [qTs[h % 2][1]], qTs[h % 2][1])
        tk.dma(tk.sp, kTs[h % 2][0][0:64, :], fm[512 + 64 * h:512 + 64 * h + 64, :], [Rfm], [kTs[h % 2][1]], kTs[h % 2][1])
    cst, Rcst = A.sb([128, 4, 128], BF16)
    mA, RmA = A.sb([128, 4, 512], BF16)
    es = [A.sb([128, 512], BF16) for _ in range(2)]
    sps = [A.sb([128, 512], BF16) for _ in range(2)]
    ws = [A.sb([128, 512], BF16) for _ in range(2)]
    accs = [A.sb([128, 512], BF16) for _ in range(2)]
    spD = {dg_: A.sb([128, 512], BF16) for dg_ in (1, 2, 3)}
    wD = {dg_: A.sb([128, 512], BF16) for dg_ in (1, 2, 3)}
    for dg_ in (1, 2, 3):
        tk.op(tk.pool, lambda e, dg_=dg_: e.memset(spD[dg_][0][:, 0:128 * dg_], 0.0), [], [spD[dg_][1]])
        tk.op(tk.pool, lambda e, dg_=dg_: e.memset(wD[dg_][0][:, 0:128 * dg_], 0.0), [], [wD[dg_][1]])

    def spbuf(i):
        dg_ = tiles[i]["dg"]
        return spD[dg_] if dg_ in (1, 2, 3) else sps[i % 2]

    def wbuf(i):
        dg_ = tiles[i]["dg"]
        return wD[dg_] if dg_ in (1, 2, 3) else ws[i % 2]

    def csl(i):
        dg_ = tiles[i]["dg"]
        return slice(128 * dg_, 512) if dg_ in (1, 2, 3) else slice(0, 512)
    osbs = [A.sb([128, 512], BF16) for _ in range(2)]
    Z1 = [A.ps([128, 512], F32) for _ in range(2)]
    Z2 = [A.ps([128, 512], F32) for _ in range(2)]
    Ob = [A.ps([128, 512], F32) for _ in range(2)]
    WB, RWB = A.ps([128, 512], F32)
    tk.dma(tk.sp, cst[:], I["cst"][:, :, :], [], [Rcst], Rcst)
    tk.dma(tk.sp, mA[:], I["maskA"][:, :, :], [], [RmA], RmA)
    tmv = tm.rearrange("(b p) c -> p b c", p=128)
    Rvas = [tk.res("va%d" % g) for g in range(8)]

    def load_v():
        for b0 in range(0, 64, 8):
            tk.dma(tk.sp, va[:, b0:b0 + 8, :], tmv[:, b0:b0 + 8, 0:512], [Rtm], [Rvas[b0 // 8]], Rvas[b0 // 8])
    ident = cst[:, 0, :]
    NTm = cst[:, 1, :]
    NOm = cst[:, 2, :]

    tiles = []
    for h in range(8):
        for qt in range(NTT):
            kbs = list(range(4 * qt + 3, -1, -1))
            for ki, kb in enumerate(kbs):
                tiles.append(dict(h=h, qt=qt, kb=kb, first=(ki == 0), last=(kb == 0),
                                  dg=(kb - 4 * qt if kb >= 4 * qt else None), newh=(qt == 0 and ki == 0)))
    n = len(tiles)
    for i, t in enumerate(tiles):
        t["ob"] = (t["h"] * NTT + t["qt"]) % 2

    def qk(t, Z, RZ, extra_stop):
        kb, t0 = t["kb"], t["qt"] * 512
        dg = t["dg"]
        qT, RqT = qTs[t["h"] % 2]
        kT, RkT = kTs[t["h"] % 2]
        tk.op(tk.pe, lambda e: e.matmul(Z[:], lhsT=kT[:, kb * 128:(kb + 1) * 128], rhs=qT[:, t0:t0 + 512],
                                        start=True, stop=(dg is None and extra_stop)), [RkT, RqT], [RZ])
        if dg is not None:
            tk.op(tk.pe, lambda e: e.matmul(Z[:], lhsT=ident, rhs=mA[:, dg, :], start=False, stop=extra_stop), [Rcst, RmA], [RZ])

    def s0(i):
        t = tiles[i]
        if t["newh"]:
            h = t["h"]
            if h == 0:
                loadqk(0)
                load_v()
            if h + 1 < 8:
                loadqk(h + 1)
            qT, RqT = qTs[h % 2]
            kT, RkT = kTs[h % 2]
            warm(cx, WB, RWB, kT[:, 0:128], RkT, qT[:, 0:512], RqT, 12)
        Z, RZ = Z1[i % 2]
        qk(t, Z, RZ, True)

    def s1a(i):
        Z, RZ = Z1[i % 2]
        ee, Re = es[i % 2]
        cs = csl(i)
        tk.op(tk.act, lambda e: e.activation(out=ee[:, cs], in_=Z[:, cs], func=AF.Exp), [RZ], [Re])

    def s1b(i):
        ee, Re = es[i % 2]
        sp, Rsp = spbuf(i)
        cs = csl(i)
        tk.op(tk.act, lambda e: e.activation(out=sp[:, cs], in_=ee[:, cs], func=AF.Ln, bias=1.0, scale=1.0), [Re], [Rsp])

    def s2(i):
        t = tiles[i]
        Z, RZ = Z2[i % 2]
        sp, Rsp = spbuf(i)
        qk(t, Z, RZ, False)
        tk.op(tk.pe, lambda e: e.matmul(Z[:], lhsT=NTm, rhs=sp[:], start=False, stop=t["first"]), [Rcst, Rsp], [RZ])
        ac, Rac = accs[i % 2]
        an, Ran = accs[(i + 1) % 2]
        if not t["first"]:
            tk.op(tk.pe, lambda e: e.matmul(Z[:], lhsT=NOm, rhs=ac[:], start=False, stop=True), [Rcst, Rac], [RZ])
        if not t["last"]:
            if t["first"]:
                tk.op(tk.dve, lambda e: e.tensor_copy(out=an[:], in_=sp[:]), [Rsp], [Ran])
            else:
                tk.op(tk.dve, lambda e: e.tensor_tensor(out=an[:], in0=ac[:], in1=sp[:], op=ALU.add), [Rac, Rsp], [Ran])

    def s3(i):
        Z, RZ = Z2[i % 2]
        w, Rw = wbuf(i)
        cs = csl(i)
        tk.op(tk.act, lambda e: e.activation(out=w[:, cs], in_=Z[:, cs], func=AF.Exp), [RZ], [Rw])

    def s4(i):
        t = tiles[i]
        w, Rw = wbuf(i)
        O, RO = Ob[t["ob"]]
        h, kb = t["h"], t["kb"]
        hp = 128 * (h // 2)
        rows = slice(64 * (h % 2), 64 * (h % 2) + 64)
        tk.op(tk.pe, lambda e: e.matmul(O[:, :], lhsT=va[:, kb, hp:hp + 128], rhs=w[:],
                                        start=t["first"], stop=t["last"]), [Rvas[kb // 8], Rw], [RO])
        if t["last"]:
            osb, Ros = osbs[t["ob"]]
            t0 = t["qt"] * 512
            tk.op(tk.dve, lambda e: e.tensor_copy(out=osb[rows, :], in_=O[rows, :]), [RO], [Ros])
            tk.dma(tk.pool, oT[64 * h:64 * h + 64, t0:t0 + 512], osb[rows, :], [Ros], [RoT], Ros)

    s0(0)
    for step in range(n + 3):
        if step + 1 < n:
            s0(step + 1)
        if step < n:
            s1a(step)
        if 0 <= step - 1 < n:
            s2(step - 1)
        if 0 <= step - 2 < n:
            s3(step - 2)
        if step < n:
            s1b(step)
        if 0 <= step - 3 < n:
            s4(step - 3)
    tk.barrier()
    A.close()


def phase_mixB(cx, fm, tm, oT, Rfm, Rtm, RoT):
    nc, tk, I = cx.nc, cx.tk, cx.I
    A = Alloc(cx, "mB")
    vb, Rvb = A.sb([128, 64, 512], BF16)
    qT, RqT = A.sb([128, 2, S], BF16)
    kT, RkT = A.sb([128, S], BF16)
    cst, Rcst = A.sb([128, 4, 128], BF16)
    onesf, Rof = A.sb([128, 128], F32)
    bias, Rbias = A.sb([128, 2, 5, 512], F32)
    tb31, Rtb = A.sb([128, 16], F32)
    lamt, Rlam = A.sb([128, 4, 64], F32)
    lsc, Rlsc = A.sb([128, 8], F32)
    gsc, Rgsc = A.sb([128, 1], F32)
    tmps = [A.sb([128, 512], F32) for _ in range(4)]
    Ps = [A.sb([128, 512], BF16) for _ in range(4)]
    cb = [A.sb([128, 512], F32) for _ in range(5)]
    osb, Ros = A.sb([128, 512], BF16)
    Zs = [A.ps([128, 512], F32) for _ in range(2)]
    U = [A.ps([128, 512], F32) for _ in range(2)]
    Lp = [A.ps([128, 512], F32) for _ in range(2)]
    SSQ, Rssq = A.ps([128, 512], F32)
    WB, RWB = A.ps([128, 512], F32)

    tk.op(tk.pool, lambda e: e.memset(qT[64:128, 0, :], 0.0), [], [RqT])
    tk.op(tk.pool, lambda e: e.memset(qT[0:64, 1, :], 0.0), [], [RqT])
    tk.dma(tk.sp, cst[:], I["cst"][:, :, :], [], [Rcst], Rcst)
    tk.dma(tk.sp, onesf[:], I["onesf"][:, :], [], [Rof], Rof)
    tk.dma(tk.sp, tb31[:], I["t5_table"][31:32, :].broadcast_to([128, 16]), [], [Rtb], Rtb)
    tk.dma(tk.sp, lamt[:], I["lam"].rearrange("(o a) d -> o a d", o=1).broadcast_to([128, 4, 64]), [], [Rlam], Rlam)
    tk.dma(tk.sp, gsc[:], I["subln_g"][:, :], [], [Rgsc], Rgsc)
    tmv = tm.rearrange("(b p) c -> p b c", p=128)
    Rvbs = [tk.res("vb%d" % g) for g in range(8)]

    def load_v():
        for b0 in range(0, 64, 8):
            tk.dma(tk.sp, vb[:, b0:b0 + 8, :], tmv[:, b0:b0 + 8, 512:1024], [Rtm], [Rvbs[b0 // 8]], Rvbs[b0 // 8])
    onesb = cst[:, 3, :]
    t1, Rt1 = cb[0]
    tk.op(tk.dve, lambda e: e.tensor_tensor(out=t1[:, 0:64], in0=lamt[:, 0, :], in1=lamt[:, 1, :], op=ALU.mult), [Rlam], [Rt1])
    tk.op(tk.dve, lambda e: e.tensor_tensor(out=t1[:, 64:128], in0=lamt[:, 2, :], in1=lamt[:, 3, :], op=ALU.mult), [Rlam], [Rt1])
    tk.op(tk.dve, lambda e: e.reduce_sum(out=lsc[:, 0:1], in_=t1[:, 0:64], axis=mybir.AxisListType.X), [Rt1], [Rlsc])
    tk.op(tk.dve, lambda e: e.reduce_sum(out=lsc[:, 1:2], in_=t1[:, 64:128], axis=mybir.AxisListType.X), [Rt1], [Rlsc])
    tk.op(tk.act, lambda e: e.activation(out=lsc[:, 2:4], in_=lsc[:, 0:2], func=AF.Exp), [Rlsc], [Rlsc])
    tk.op(tk.dve, lambda e: e.scalar_tensor_tensor(out=lsc[:, 4:5], in0=lsc[:, 3:4], scalar=-LAMBDA_INIT0, in1=lsc[:, 2:3],
                                                   op0=ALU.add, op1=ALU.subtract), [Rlsc], [Rlsc])
    tk.op(tk.dve, lambda e: e.tensor_scalar_mul(out=gsc[:], in0=gsc[:], scalar1=1.0 - LAMBDA_INIT0), [Rgsc], [Rgsc])

    tiles = []
    for h in range(4):
        for qt in range(NTT):
            for m in range(2):
                near = [kb for kb in range(4 * qt + 3, -1, -1) if kb - 4 * qt + 1 >= 0]
                far = [kb for kb in range(4 * qt + 3, -1, -1) if kb - 4 * qt + 1 < 0]
                tot = len(near) + len(far)
                pos = set(int((j + 0.5) * tot / len(near)) for j in range(len(near)))
                kbs = []
                ni, fi = 0, 0
                for k in range(tot):
                    if (k in pos and ni < len(near)) or fi >= len(far):
                        kbs.append(near[ni]); ni += 1
                    else:
                        kbs.append(far[fi]); fi += 1
                for ki, kb in enumerate(kbs):
                    ib = kb - 4 * qt + 1
                    tiles.append(dict(h=h, qt=qt, m=m, kb=kb, first=(ki == 0), last=(ki == tot - 1),
                                      ib=(ib if ib >= 0 else None), newh=(qt == 0 and m == 0 and ki == 0)))
    n = len(tiles)
    dqueue = []
    pend_reads = [0, 0]

    def dq(fn, mtag=None):
        dqueue.append((fn, mtag))
        if mtag is not None:
            pend_reads[mtag] += 1

    def dpop():
        fn, mtag = dqueue.pop(0)
        if mtag is not None:
            pend_reads[mtag] -= 1
        fn()

    def s0(i):
        t = tiles[i]
        h, m, kb, t0 = t["h"], t["m"], t["kb"], t["qt"] * 512
        if t["newh"]:
            tk.dma(tk.sp, qT[0:64, 0, :], fm[1024 + 128 * h:1024 + 128 * h + 64, :], [Rfm], [RqT], RqT)
            tk.dma(tk.sp, qT[64:128, 1, :], fm[1024 + 128 * h + 64:1024 + 128 * h + 128, :], [Rfm], [RqT], RqT)
            tk.dma(tk.sp, kT[:], fm[1536 + 128 * h:1536 + 128 * h + 128, :], [Rfm], [RkT], RkT)
            for mm in range(2):
                tk.dma(tk.sp, bias[:, mm, :, :], I["biasB"][2 * h + mm], [], [Rbias], Rbias)
            if h == 0:
                load_v()
            for mm in range(2):
                sl = 8 + 2 * h + mm
                for ib_ in range(5):
                    tk.op(tk.dve, lambda e, mm=mm, ib_=ib_, sl=sl: e.tensor_scalar(
                        out=bias[:, mm, ib_, :], in0=bias[:, mm, ib_, :], scalar1=tb31[:, sl:sl + 1], scalar2=None, op0=ALU.subtract),
                        [Rbias, Rtb], [Rbias])
            warm(cx, WB, RWB, kT[:, 0:128], RkT, qT[:, 0, 0:512], RqT, 12)
        Z, RZ = Zs[i % 2]
        tk.op(tk.pe, lambda e: e.matmul(Z[:], lhsT=kT[:, kb * 128:(kb + 1) * 128], rhs=qT[:, m, t0:t0 + 512],
                                        start=True, stop=True), [RkT, RqT], [RZ])

    def s1(i):
        t = tiles[i]
        h, m = t["h"], t["m"]
        Z, RZ = Zs[i % 2]
        P, RP = Ps[i % 4]
        if t["ib"] is not None:
            tmp, Rtmp = tmps[i % 4]
            tk.op(tk.dve, lambda e: e.tensor_tensor(out=tmp[:], in0=Z[:], in1=bias[:, m, t["ib"], :], op=ALU.add), [RZ, Rbias], [Rtmp])
            tk.op(tk.act, lambda e: e.activation(out=P[:], in_=tmp[:], func=AF.Exp), [Rtmp], [RP])
        else:
            tk.op(tk.act, lambda e: e.activation(out=P[:], in_=Z[:], func=AF.Exp), [RZ], [RP])

    def s2(i):
        t = tiles[i]
        h, m, kb = t["h"], t["m"], t["kb"]
        P, RP = Ps[i % 4]
        Um, RU = U[m]
        Lm, RL = Lp[m]
        if t["first"]:
            while pend_reads[m] > 0:
                dpop()
            pass
        tk.op(tk.pe, lambda e: e.matmul(Lm[:], lhsT=onesb, rhs=P[:], start=t["first"], stop=t["last"]), [Rcst, RP], [RL])
        tk.op(tk.pe, lambda e: e.matmul(Um[:], lhsT=vb[:, kb, 128 * h:128 * h + 128], rhs=P[:], start=t["first"], stop=t["last"]), [Rvbs[kb // 8], RP], [RU])
        if t["last"]:
            combineM(m)
            if m == 1:
                combineA(t)
                combineB(t)

    def combineM(m):
        (r0, Rr0), (o0, Ro0), (r1, Rr1), (o1, Ro1), (od, Rod) = cb
        rr, Rrr = (r0, Rr0) if m == 0 else (r1, Rr1)
        oo, Roo = (o0, Ro0) if m == 0 else (o1, Ro1)
        dq(lambda: tk.op(tk.act, lambda e: e.activation(out=rr[:], in_=Lp[m][0][:], func=AF.Ln), [Lp[m][1]], [Rrr]), m)
        dq(lambda: tk.op(tk.dve, lambda e: e.tensor_copy(out=oo[:], in_=U[m][0][:]), [U[m][1]], [Roo]), m)
        dq(lambda: tk.op(tk.act, lambda e: e.activation(out=rr[:], in_=rr[:], func=AF.Exp, scale=-1.0), [Rrr], [Rrr]))
        dq(lambda: tk.op(tk.dve, lambda e: e.tensor_tensor(out=oo[:], in0=oo[:], in1=rr[:], op=ALU.mult), [Roo, Rrr], [Roo]))

    def combineA(t):
        (r0, Rr0), (o0, Ro0), (r1, Rr1), (o1, Ro1), (od, Rod) = cb
        dq(lambda: tk.op(tk.dve, lambda e: e.scalar_tensor_tensor(out=od[:], in0=o1[:], scalar=lsc[:, 4:5], in1=o0[:], op0=ALU.mult, op1=ALU.add),
              [Ro1, Ro0, Rlsc], [Rod]))
        dq(lambda: tk.op(tk.pool, lambda e: e.tensor_tensor(out=r0[:], in0=od[:], in1=od[:], op=ALU.mult), [Rod], [Rr0]))

    def combineB(t):
        h, t0 = t["h"], t["qt"] * 512
        (r0, Rr0), (o0, Ro0), (r1, Rr1), (o1, Ro1), (od, Rod) = cb
        dq(lambda: tk.op(tk.pe, lambda e: e.matmul(SSQ[:], lhsT=onesf[:], rhs=r0[:], start=True, stop=True), [Rof, Rr0], [Rssq]))
        dq(lambda: tk.op(tk.act, lambda e: e.activation(out=r1[:], in_=SSQ[:], func=AF.Ln, scale=1.0 / 128, bias=1e-5), [Rssq], [Rr1]))
        dq(lambda: tk.op(tk.act, lambda e: e.activation(out=o1[:], in_=r1[:], func=AF.Exp, scale=-0.5), [Rr1], [Ro1]))
        dq(lambda: tk.op(tk.dve, lambda e: e.scalar_tensor_tensor(out=osb[:], in0=od[:], scalar=gsc[:, 0:1], in1=o1[:], op0=ALU.mult, op1=ALU.mult),
              [Rod, Rgsc, Ro1], [Ros]))
        dq(lambda: tk.dma(tk.pool, oT[512 + 128 * h:512 + 128 * h + 128, t0:t0 + 512], osb[:], [Ros], [RoT], Ros))

    s0(0)
    for step in range(n + 3):
        if 0 <= step - 2 < n:
            s2(step - 2)
            if (step % 128) == 0 and not tiles[step - 2]["first"]:
                warm(cx, WB, RWB, kT[:, 0:128], RkT, qT[:, 0, 0:512], RqT, 6)
        if step + 1 < n:
            s0(step + 1)
        if step < n:
            s1(step)
        for _ in range(2 if len(dqueue) > 12 else 1):
            if dqueue:
                dpop()
    while dqueue:
        dpop()
    tk.barrier()
    A.close()


def phase_mixC(cx, fm, tm, oT, Rfm, Rtm, RoT):
    nc, tk, I = cx.nc, cx.tk, cx.I
    A = Alloc(cx, "mC")
    qTs = [A.sb([128, S], BF16) for _ in range(2)]
    kTs = [A.sb([128, S], BF16) for _ in range(2)]
    vps = [A.sb([128, 64, 128], BF16) for _ in range(3)]
    cst, Rcst = A.sb([128, 4, 128], BF16)
    bias, Rbias = A.sb([128, 2, 3, 2, 128], F32)
    Uacc, RUa = A.sb([128, S], F32)
    Lacc, RLa = A.sb([128, S], F32)
    tmps = [A.sb([128, 4, 128], F32) for _ in range(4)]
    Ps = [A.sb([128, 512], BF16) for _ in range(4)]
    rl, Rrl = A.sb([128, 512], F32)
    osbs = [A.sb([128, 512], BF16) for _ in range(2)]
    Zs = [A.ps([128, 512], F32) for _ in range(4)]
    Us = [A.ps([128, 512], F32) for _ in range(2)]
    Ls = [A.ps([128, 512], F32) for _ in range(2)]
    tk.dma(tk.sp, cst[:], I["cst"][:, :, :], [], [Rcst], Rcst)
    onesb = cst[:, 3, :]
    RS = (1, 4, 16)
    mi = 0
    def loadqk(j):
        q_, Rq_ = qTs[j % 2]
        k_, Rk_ = kTs[j % 2]
        tk.dma(tk.sp, q_[:], fm[128 * j:128 * j + 128, :], [Rfm], [Rq_], Rq_)
        tk.dma(tk.sp, k_[:], fm[1024 + 128 * j:1024 + 128 * j + 128, :], [Rfm], [Rk_], Rk_)

    loadqk(0)
    for j in range(8):
        qT, RqT = qTs[j % 2]
        kT, RkT = kTs[j % 2]
        for hh in range(2):
            tk.dma(tk.sp, bias[:, hh, :, :, :], I["biasC"][2 * j + hh], [], [Rbias], Rbias)
        for ri, r in enumerate(RS):
            vp, Rvp = vps[ri]
            nb = 64 // r
            src = tm.rearrange("(b p c) d -> p c b d", p=128, c=r)
            dst = vp[:].rearrange("p (c b) d -> p c b d", c=r)
            for c in range(r):
                for b0 in range(0, nb, 16):
                    b1 = min(nb, b0 + 16)
                    tk.dma(tk.sp, dst[:, c, b0:b1, :], src[:, c, b0:b1, 128 * j:128 * j + 128], [Rtm], [Rvp], Rvp)
        if j + 1 < 8:
            loadqk(j + 1)
        mts = []
        for hh in range(2):
            for ri, r in enumerate(RS):
                nb = 64 // r
                for mt in range(16):
                    c = (4 * mt) // nb
                    b0 = (4 * mt) % nb
                    mts.append(dict(hh=hh, ri=ri, r=r, nb=nb, c=c, b0=b0, k=mi,
                                    has_prev=[(b0 + u) >= 1 for u in range(4)]))
                    mi += 1

        def cols(m, bq):
            st_ = m["c"] + m["r"] * 128 * bq
            return slice(st_, st_ + m["r"] * 127 + 1, m["r"])

        def c0(m):
            pr = slice(64 * m["hh"], 64 * m["hh"] + 64)
            Zd, RZd = Zs[(2 * m["k"]) % 4]
            Zp, RZp = Zs[(2 * m["k"] + 1) % 4]
            for u in range(4):
                bq = m["b0"] + u
                tk.op(tk.pe, lambda e, u=u, bq=bq: e.matmul(Zd[:, 128 * u:128 * u + 128], lhsT=kT[pr, cols(m, bq)], rhs=qT[pr, cols(m, bq)],
                                                            start=True, stop=True), [RkT, RqT], [RZd])
            for u in range(4):
                bq = m["b0"] + u
                if m["has_prev"][u]:
                    tk.op(tk.pe, lambda e, u=u, bq=bq: e.matmul(Zp[:, 128 * u:128 * u + 128], lhsT=kT[pr, cols(m, bq - 1)], rhs=qT[pr, cols(m, bq)],
                                                                start=True, stop=True), [RkT, RqT], [RZp])

        def c1(m):
            hh, ri = m["hh"], m["ri"]
            Zd, RZd = Zs[(2 * m["k"]) % 4]
            Zp, RZp = Zs[(2 * m["k"] + 1) % 4]
            td, Rtd = tmps[(2 * m["k"]) % 4]
            tp, Rtp = tmps[(2 * m["k"] + 1) % 4]
            Pd, RPd = Ps[(2 * m["k"]) % 4]
            Pp, RPp = Ps[(2 * m["k"] + 1) % 4]
            u0 = 0 if m["has_prev"][0] else 1
            tk.op(tk.dve, lambda e: e.tensor_tensor(out=td[:], in0=Zd[:].rearrange("p (u t) -> p u t", u=4),
                                                    in1=bias[:, hh, ri, 0:1, :].broadcast_to([128, 4, 128]), op=ALU.add),
                  [RZd, Rbias], [Rtd])
            tk.op(tk.act, lambda e: e.activation(out=Pd[:], in_=td[:].rearrange("p u t -> p (u t)"), func=AF.Exp), [Rtd], [RPd])
            tk.op(tk.dve, lambda e: e.tensor_tensor(out=tp[:, u0:4, :], in0=Zp[:, 128 * u0:512].rearrange("p (u t) -> p u t", u=4 - u0),
                                                    in1=bias[:, hh, ri, 1:2, :].broadcast_to([128, 4 - u0, 128]), op=ALU.add),
                  [RZp, Rbias], [Rtp])
            tk.op(tk.act, lambda e: e.activation(out=Pp[:, 128 * u0:512], in_=tp[:, u0:4, :].rearrange("p u t -> p (u t)"), func=AF.Exp),
                  [Rtp], [RPp])

        def c2(m):
            vp, Rvp = vps[m["ri"]]
            Pd, RPd = Ps[(2 * m["k"]) % 4]
            Pp, RPp = Ps[(2 * m["k"] + 1) % 4]
            Ub, RUb = Us[m["k"] % 2]
            Lb, RLb = Ls[m["k"] % 2]
            for u in range(4):
                B = m["c"] * m["nb"] + m["b0"] + u
                hp = m["has_prev"][u]
                us = slice(128 * u, 128 * u + 128)
                tk.op(tk.pe, lambda e, B=B, us=us, hp=hp: e.matmul(Ub[:, us], lhsT=vp[:, B, :], rhs=Pd[:, us], start=True, stop=not hp),
                      [Rvp, RPd], [RUb])
                if hp:
                    tk.op(tk.pe, lambda e, B=B, us=us: e.matmul(Ub[:, us], lhsT=vp[:, B - 1, :], rhs=Pp[:, us], start=False, stop=True),
                          [Rvp, RPp], [RUb])
                tk.op(tk.pe, lambda e, us=us, hp=hp: e.matmul(Lb[:, us], lhsT=onesb, rhs=Pd[:, us], start=True, stop=not hp),
                      [Rcst, RPd], [RLb])
                if hp:
                    tk.op(tk.pe, lambda e, us=us: e.matmul(Lb[:, us], lhsT=onesb, rhs=Pp[:, us], start=False, stop=True),
                          [Rcst, RPp], [RLb])

        def c3(m):
            pr = slice(64 * m["hh"], 64 * m["hh"] + 64)
            r = m["r"]
            Ub, RUb = Us[m["k"] % 2]
            Lb, RLb = Ls[m["k"] % 2]
            st_ = m["c"] + r * 128 * m["b0"]
            dcol = slice(st_, st_ + r * 511 + 1, r)
            if m["ri"] == 0:
                tk.op(tk.act, lambda e: e.copy(out=Uacc[pr, dcol], in_=Ub[pr, :]), [RUb], [RUa])
                tk.op(tk.act, lambda e: e.copy(out=Lacc[pr, dcol], in_=Lb[pr, :]), [RLb], [RLa])
            else:
                tk.op(tk.dve, lambda e: e.tensor_tensor(out=Uacc[pr, dcol], in0=Ub[pr, :], in1=Uacc[pr, dcol], op=ALU.add), [RUb, RUa], [RUa])
                tk.op(tk.dve, lambda e: e.tensor_tensor(out=Lacc[pr, dcol], in0=Lb[pr, :], in1=Lacc[pr, dcol], op=ALU.add), [RLb, RLa], [RLa])

        nm = len(mts)
        warm(cx, Us[0][0], Us[0][1], kT[0:64, 0:128], RkT, qT[0:64, 0:512], RqT, 10)
        c0(mts[0])
        for step in range(nm + 2):
            if step + 1 < nm:
                c0(mts[step + 1])
            if step < nm:
                c1(mts[step])
            if 0 <= step - 1 < nm:
                c2(mts[step - 1])
            if 0 <= step - 2 < nm:
                c3(mts[step - 2])
        for ch in range(NTT):
            cs = slice(512 * ch, 512 * ch + 512)
            osb, Ros = osbs[ch % 2]
            tk.op(tk.act, lambda e: e.activation(out=rl[:], in_=Lacc[:, cs], func=AF.Ln), [RLa], [Rrl])
            tk.op(tk.act, lambda e: e.activation(out=rl[:], in_=rl[:], func=AF.Exp, scale=-1.0), [Rrl], [Rrl])
            tk.op(tk.dve, lambda e: e.tensor_tensor(out=osb[:], in0=Uacc[:, cs], in1=rl[:], op=ALU.mult), [RUa, Rrl], [Ros])
            tk.dma(tk.pool, oT[128 * j:128 * j + 128, cs], osb[:], [Ros], [RoT], Ros)
    tk.barrier()
    A.close()


def phase_tail(cx, L, hsrc, Rhsrc, oT, RoT, hdst, Rhdst, Rwb, final):
    nc, tk, I, Wb = cx.nc, cx.tk, cx.I, cx.Wb
    A = Alloc(cx, "tl%d" % L)
    NSLOT = 6
    slots = [A.sb([128, 4096], BF16) for _ in range(NSLOT)]
    gts = [A.sb([128, D], F32) for _ in range(3 if final else 2)]
    ident, Rid = A.sb([128, 128], BF16)
    hts = [A.sb([128, 4, D], F32) for _ in range(2)]
    ots = [A.sb([128, 8, 512], BF16) for _ in range(2)]
    pts = [A.sb([128, 4, 256], F32) for _ in range(2)]
    junk, Rjunk = A.sb([128, D], BF16)
    sts = [A.sb([128, 8], F32) for _ in range(3)]
    hns = [A.sb([128, 4, D], BF16) for _ in range(2)]
    hnTs = [A.sb([128, 8, 512], BF16) for _ in range(2)]
    uT, RuT = A.sb([128, 32, 512], BF16)
    rls = [A.sb([128, 512], F32) for _ in range(2)]
    gsb = [A.sb([128, 512], F32) for _ in range(2)]
    pbf, Rpbf = A.sb([128, 4, 256], BF16)
    pT, RpT = A.sb([128, 2, 512], BF16)
    psTs = Rot([A.ps([128, 1024], BF16) for _ in range(2)])
    banks = Rot([A.ps([128, 512], F32) for _ in range(6)])

    tk.dma(tk.sp, ident[:], I["cst"][:, 0, :], [], [Rid], Rid)
    tk.dma(tk.sp, gts[0][0][:], I["norm_mlp_g"][L:L + 1, :].broadcast_to([128, D]), [], [gts[0][1]], gts[0][1])
    tk.dma(tk.sp, gts[1][0][:], I["norm_ple_g"][L:L + 1, :].broadcast_to([128, D]), [], [gts[1][1]], gts[1][1])
    if final:
        tk.dma(tk.sp, gts[2][0][:], I["final_norm_g"][0:1, :].broadcast_to([128, D]), [], [gts[2][1]], gts[2][1])

    wo = Wb["out", L].rearrange("(c p) n -> p c n", p=128)
    wu = Wb["up", L].rearrange("(c p) n -> p c n", p=128)
    wd = Wb["down", L].rearrange("(f p) n -> p f n", p=128)
    wg = Wb["gate", L].rearrange("(c p) n -> p c n", p=128)
    wp_ = Wb["ple", L].rearrange("(c p) n -> p c n", p=128)

    def p_out():
        return [("out", wo[:, 4 * q:4 * q + 4, :], (4, 1024)) for q in range(2)]

    plist = p_out()
    for tt in range(NTT):
        plist += [("up", wu[:, :, 512 * q:512 * q + 512], (8, 512)) for q in range(8)]
        for half in range(2):
            plist += [("down", wd[:, 8 * q:8 * q + 8, 512 * half:512 * half + 512], (8, 512)) for q in range(4)]
        if tt + 1 < NTT:
            plist += p_out()
        plist += [("gate", wg[:, 4 * q:4 * q + 4, :], (4, 1024)) for q in range(2)]
        plist += [("ple", wp_[:, :, :], (2, 1024))]
    total = len(plist)
    state = dict(issued=0, taken=0)

    def issue_upto(k):
        while state["issued"] < min(k, total):
            g = state["issued"]
            nm, ap, (a, b) = plist[g]
            sl, Rsl = slots[g % NSLOT]
            tk.dma(tk.sp, sl[:, 0:a * b].rearrange("p (a b) -> p a b", a=a), ap, [Rwb], [Rsl], Rsl)
            state["issued"] += 1

    def take(kind):
        g = state["taken"]
        state["taken"] += 1
        assert g < state["issued"], (g, state)
        nm, ap, (a, b) = plist[g]
        assert nm == kind, (nm, kind)
        sl, Rsl = slots[g % NSLOT]
        return g, sl[:, 0:a * b].rearrange("p (a b) -> p a b", a=a), Rsl

    def done(g):
        issue_upto(g + NSLOT + 1)

    def load(tt):
        t0 = tt * 512
        ht, Rht = hts[tt % 2]
        ot, Rot_ = ots[tt % 2]
        pt, Rpt = pts[tt % 2]
        tk.dma(tk.sp, ht[:], hsrc[t0:t0 + 512, :].rearrange("(j p) d -> p j d", p=128), [Rhsrc] if Rhsrc is not None else [], [Rht], Rht)
        tk.dma(tk.sp, ot[:], oT[:, t0:t0 + 512].rearrange("(c p) t -> p c t", p=128), [RoT], [Rot_], Rot_)
        tk.dma(tk.sp, pt[:], I["p"][L, t0:t0 + 512, :].rearrange("(j p) d -> p j d", p=128), [], [Rpt], Rpt)

    def stageA(tt):
        ht, Rht = hts[tt % 2]
        ot, Rot_ = ots[tt % 2]
        g0, w0, Rw0 = take("out")
        g1, w1, Rw1 = take("out")
        for j in range(4):
            for half in range(2):
                bk, Rb = banks.next()
                for c in range(8):
                    wv, Rwv = (w0, Rw0) if c < 4 else (w1, Rw1)
                    tk.op(tk.pe, lambda e, c=c, wv=wv, bk=bk, j=j, half=half: e.matmul(
                        bk[:], lhsT=ot[:, c, j * 128:(j + 1) * 128], rhs=wv[:, c % 4, 512 * half:512 * half + 512],
                        start=(c == 0), stop=(c == 7)), [Rot_, Rwv], [Rb])
                hs = ht[:, j, 512 * half:512 * half + 512]
                tk.op(tk.dve, lambda e, bk=bk, hs=hs: e.tensor_tensor(out=hs, in0=bk[:], in1=hs, op=ALU.add), [Rb, Rht], [Rht])
        done(g1)

    def stageN(tt, which):
        ht, Rht = hts[tt % 2]
        st, Rst = sts[which]
        hn, Rhn = hns[which]
        gt, Rg = gts[which]
        rms_stats(cx, ht, Rht, st, Rst, junk, Rjunk, 4, 1e-6, D)
        for j in range(4):
            tk.op(tk.dve, lambda e, j=j: e.scalar_tensor_tensor(out=hn[:, j, :], in0=ht[:, j, :], scalar=st[:, 4 + j:5 + j],
                                                                in1=gt[:], op0=ALU.mult, op1=ALU.mult), [Rht, Rst, Rg], [Rhn])

    def stageT(tt, which):
        hn, Rhn = hns[which]
        hnT, RhnT = hnTs[which]
        for j in range(4):
            psT, RpsT = psTs.next()
            for c in range(8):
                tk.op(tk.pe, lambda e, j=j, c=c, psT=psT: e.transpose(out=psT[:, c * 128:(c + 1) * 128],
                                                                     in_=hn[:, j, c * 128:(c + 1) * 128], identity=ident[:]),
                      [Rhn, Rid], [RpsT])
            src = psT[:].rearrange("p (c t) -> p c t", c=8)
            dst = hnT[:, :, j * 128:(j + 1) * 128]
            if j % 2 == 0:
                tk.op(tk.act, lambda e, src=src, dst=dst: e.copy(out=dst, in_=src), [RpsT], [RhnT])
            else:
                tk.op(tk.dve, lambda e, src=src, dst=dst: e.tensor_copy(out=dst, in_=src), [RpsT], [RhnT])

    def stageC(tt):
        hnT, RhnT = hnTs[0]
        for q in range(8):
            g, wv, Rwv = take("up")
            for fl in range(4):
                f = 4 * q + fl
                bk, Rb = banks.next()
                for c in range(8):
                    tk.op(tk.pe, lambda e, c=c, wv=wv, bk=bk, fl=fl: e.matmul(bk[:], lhsT=wv[:, c, 128 * fl:128 * fl + 128], rhs=hnT[:, c, :],
                                                                              start=(c == 0), stop=(c == 7)), [Rwv, RhnT], [Rb])
                rl, Rrl = rls[f % 2]
                tk.op(tk.act, lambda e, bk=bk, rl=rl: e.activation(out=rl[:], in_=bk[:], func=AF.Relu), [Rb], [Rrl])
                tk.op(tk.dve, lambda e, rl=rl, f=f: e.tensor_tensor(out=uT[:, f, :], in0=rl[:], in1=rl[:], op=ALU.mult), [Rrl], [RuT])
            done(g)

    def stageD(tt):
        ht, Rht = hts[tt % 2]
        for half in range(2):
            bks = [banks.next() for _ in range(4)]
            for q in range(4):
                g, wv, Rwv = take("down")
                for j in range(4):
                    bk, Rb = bks[j]
                    for fl in range(8):
                        f = 8 * q + fl
                        tk.op(tk.pe, lambda e, bk=bk, wv=wv, fl=fl, f=f, j=j: e.matmul(bk[:], lhsT=uT[:, f, j * 128:(j + 1) * 128], rhs=wv[:, fl, :],
                                                                                      start=(f == 0), stop=(f == 31)), [RuT, Rwv], [Rb])
                done(g)
            for j in range(4):
                bk, Rb = bks[j]
                hs = ht[:, j, 512 * half:512 * half + 512]
                tk.op(tk.dve, lambda e, bk=bk, hs=hs: e.tensor_tensor(out=hs, in0=bk[:], in1=hs, op=ALU.add), [Rb, Rht], [Rht])

    evc = [0]

    def stageP(tt):
        pt, Rpt = pts[tt % 2]
        tk.op(tk.pool, lambda e: e.tensor_copy(out=pbf[:], in_=pt[:]), [Rpt], [Rpbf])
        for j in range(4):
            psT, RpsT = psTs.next()
            for c in range(2):
                tk.op(tk.pe, lambda e, j=j, c=c, psT=psT: e.transpose(out=psT[:, c * 128:(c + 1) * 128], in_=pbf[:, j, c * 128:(c + 1) * 128],
                                                                     identity=ident[:]), [Rpbf, Rid], [RpsT])
            tk.op(tk.act, lambda e, j=j, psT=psT: e.copy(out=pT[:, :, j * 128:(j + 1) * 128],
                                                         in_=psT[:, 0:256].rearrange("p (c t) -> p c t", c=2)), [RpsT], [RpT])

    def stageE(tt):
        ht, Rht = hts[tt % 2]
        hnT, RhnT = hnTs[1]
        g18, wg0, Rwg0 = take("gate")
        g19, wg1, Rwg1 = take("gate")
        g20, wp, Rwp = take("ple")
        for j in range(4):
            for half in range(2):
                bk, Rb = banks.next()
                for c in range(8):
                    wv, Rwv = (wg0, Rwg0) if c < 4 else (wg1, Rwg1)
                    tk.op(tk.pe, lambda e, c=c, wv=wv, bk=bk, j=j, half=half: e.matmul(
                        bk[:], lhsT=hnT[:, c, j * 128:(j + 1) * 128], rhs=wv[:, c % 4, 512 * half:512 * half + 512],
                        start=(c == 0), stop=(c == 7)), [RhnT, Rwv], [Rb])
                evc[0] += 1
                gs, Rgs = gsb[evc[0] % 2]
                tk.op(tk.act, lambda e, bk=bk, gs=gs: e.activation(out=gs[:], in_=bk[:], func=AF.Sigmoid), [Rb], [Rgs])
                bk2, Rb2 = banks.next()
                for c in range(2):
                    tk.op(tk.pe, lambda e, c=c, bk2=bk2, j=j, half=half: e.matmul(
                        bk2[:], lhsT=pT[:, c, j * 128:(j + 1) * 128], rhs=wp[:, c, 512 * half:512 * half + 512],
                        start=(c == 0), stop=(c == 1)), [RpT, Rwp], [Rb2])
                tk.op(tk.dve, lambda e, bk2=bk2, gs=gs: e.tensor_tensor(out=gs[:], in0=bk2[:], in1=gs[:], op=ALU.mult), [Rb2, Rgs], [Rgs])
                hs = ht[:, j, 512 * half:512 * half + 512]
                tk.op(tk.pool, lambda e, gs=gs, hs=hs: e.tensor_tensor(out=hs, in0=gs[:], in1=hs, op=ALU.add), [Rgs, Rht], [Rht])
        done(g20)

    def stageF(tt):
        t0 = tt * 512
        ht, Rht = hts[tt % 2]
        if final:
            st, Rst = sts[2]
            rms_stats(cx, ht, Rht, st, Rst, junk, Rjunk, 4, 1e-6, D)
            for j in range(4):
                tk.op(tk.dve, lambda e, j=j: e.scalar_tensor_tensor(out=ht[:, j, :], in0=ht[:, j, :], scalar=st[:, 4 + j:5 + j],
                                                                    in1=gts[2][0][:], op0=ALU.mult, op1=ALU.mult),
                      [Rht, Rst, gts[2][1]], [Rht])
        tk.dma(tk.pool, hdst[t0:t0 + 512, :].rearrange("(j p) d -> p j d", p=128), ht[:], [Rht], [Rhdst], Rht)

    issue_upto(2)
    load(0)
    issue_upto(NSLOT)
    stageA(0)
    stageN(0, 0)
    stageT(0, 0)
    for tt in range(NTT):
        nxt = tt + 1 < NTT
        if nxt:
            load(tt + 1)
        stageC(tt)
        stageP(tt)
        stageD(tt)
        stageN(tt, 1)
        if nxt:
            stageA(tt + 1)
        stageT(tt, 1)
        if nxt:
            stageN(tt + 1, 0)
        stageE(tt)
        if nxt:
            stageT(tt + 1, 0)
        stageF(tt)
    assert state["taken"] == total
    tk.barrier()
    A.close()


_CACHE = {}


def make_in_maps(inputs):
    f = lambda a: np.ascontiguousarray(np.asarray(a, dtype=np.float32))
    hc = host_consts(inputs["t5_table"])
    shared = {
        "t5_table": f(inputs["t5_table"]),
        "w_in_even": f(inputs["w_in_even"][0]),
        "w_in_odd": f(inputs["w_in_odd"][0]),
        "w_out": f(np.stack([np.asarray(inputs["w_out_even"][0]), np.asarray(inputs["w_out_odd"][0])])),
        "lam": f(np.concatenate([np.asarray(inputs[k]) for k in ("lambda_q1", "lambda_k1", "lambda_q2", "lambda_k2")], axis=0)),
        "subln_g": f(np.asarray(inputs["subln_g"]).reshape(128, 1)),
        "norm_mix_g": f(inputs["norm_mix_g"]),
        "norm_mlp_g": f(inputs["norm_mlp_g"]),
        "norm_ple_g": f(inputs["norm_ple_g"]),
        "final_norm_g": f(np.asarray(inputs["final_norm_g"]).reshape(1, D)),
        "w_mlp_up": f(inputs["w_mlp_up"]),
        "w_mlp_down": f(inputs["w_mlp_down"]),
        "w_ple_gate": f(inputs["w_ple_gate"]),
        "w_ple_proj": f(inputs["w_ple_proj"]),
        "cst": hc["cst"], "onesf": hc["onesf"], "maskA": hc["maskA"], "biasB": hc["biasB"], "biasC": hc["biasC"],
    }
    x = np.asarray(inputs["x"], dtype=np.float32)
    p = np.asarray(inputs["p"], dtype=np.float32)
    maps = []
    for b in range(8):
        m = dict(shared)
        m["x"] = np.ascontiguousarray(x[b])
        m["p"] = np.ascontiguousarray(p[:, b])
        maps.append(m)
    return maps


def kernel(**inputs):
    if "nc" not in _CACHE:
        _CACHE["nc"] = build_program()
    nc = _CACHE["nc"]
    maps = make_in_maps(inputs)
    res = run_bass_kernel_spmd(nc, maps, core_ids=list(range(8)))
    return np.stack([np.asarray(r["out"], dtype=np.float32) for r in res.results], axis=0)
```
